# Optimizing a Trainium2 kernel written in Bass

```python
import jax, jax.numpy as jnp
from jax import lax
import numpy as np

D_MODEL = 2048
BATCH = 4
SEQ = 4096
DEPTH = 2

CHUNK = 64
N_MIXERS = 2
RMS_EPS = 1e-5

GLA_HEADS = 4
GLA_KEY_DIM = D_MODEL // 2
GLA_VAL_DIM = D_MODEL
GLA_HEAD_K = GLA_KEY_DIM // GLA_HEADS
GLA_HEAD_V = GLA_VAL_DIM // GLA_HEADS
GLA_GATE_RANK = 16
GLA_GATE_NORMALIZER = 16.0
GLA_IN = 2 * GLA_KEY_DIM + 2 * GLA_VAL_DIM + GLA_GATE_RANK

SSD_INNER = 2 * D_MODEL
SSD_HEAD_DIM = 64
SSD_HEADS = SSD_INNER // SSD_HEAD_DIM
SSD_GROUPS = 8
SSD_HEADS_PER_GROUP = SSD_HEADS // SSD_GROUPS
SSD_STATE = 128
SSD_CONV = 4
SSD_CONV_DIM = SSD_INNER + 2 * SSD_GROUPS * SSD_STATE
SSD_IN = SSD_INNER + SSD_CONV_DIM + SSD_HEADS

D_FF = 4 * D_MODEL

kernel_name = "hybrid_gla_ssd_sqrelu_trunk"


def rmsnorm(x, w):
    xf = x.astype(jnp.float32)
    y = xf * lax.rsqrt(jnp.mean(xf * xf, axis=-1, keepdims=True) + RMS_EPS)
    return (y * w.astype(jnp.float32)).astype(x.dtype)


def to_chunks(t):
    b, tl = t.shape[:2]
    return jnp.moveaxis(t.reshape(b, tl // CHUNK, CHUNK, *t.shape[2:]), 1, 0)


def from_chunks(t):
    nc, b, c = t.shape[:3]
    return jnp.moveaxis(t, 0, 1).reshape(b, nc * c, *t.shape[3:])


def gla_chunk_scan(q, k, v, gk):
    b_sz, _, h, dk = q.shape
    dv = v.shape[-1]
    causal = jnp.tril(jnp.ones((CHUNK, CHUNK), dtype=bool))

    def step(S, inp):
        qc, kc, vc, gc = inp
        bcum = jnp.cumsum(gc, axis=1)
        b_last = bcum[:, -1]
        o_inter = jnp.einsum('bihd,bhde->bihe', qc * jnp.exp(bcum), S)
        diff = bcum[:, :, None] - bcum[:, None, :]
        decay = jnp.exp(jnp.where(causal[None, :, :, None, None], diff, -jnp.inf))
        att = jnp.einsum('bihd,bjhd,bijhd->bhij', qc, kc, decay)
        o_intra = jnp.einsum('bhij,bjhe->bihe', att, vc)
        k_dec = kc * jnp.exp(b_last[:, None] - bcum)
        S = jnp.exp(b_last)[..., None] * S + jnp.einsum('bjhd,bjhe->bhde', k_dec, vc)
        return S, o_inter + o_intra

    S0 = jnp.zeros((b_sz, h, dk, dv), jnp.float32)
    _, o = lax.scan(step, S0, (to_chunks(q), to_chunks(k), to_chunks(v), to_chunks(gk)))
    return from_chunks(o)


def gla_mixer(u, w_in, w_gk_up, b_gk_up, o_norm_w, w_out):
    b_sz, tl, _ = u.shape
    proj = u @ w_in
    q, k, v, g, gr = jnp.split(
        proj, [GLA_KEY_DIM, 2 * GLA_KEY_DIM, 2 * GLA_KEY_DIM + GLA_VAL_DIM,
               2 * GLA_KEY_DIM + 2 * GLA_VAL_DIM], axis=-1)
    gk = jax.nn.log_sigmoid((gr @ w_gk_up + b_gk_up).astype(jnp.float32)) / GLA_GATE_NORMALIZER
    q = q.reshape(b_sz, tl, GLA_HEADS, GLA_HEAD_K).astype(jnp.float32) * (GLA_HEAD_K ** -0.5)
    k = k.reshape(b_sz, tl, GLA_HEADS, GLA_HEAD_K).astype(jnp.float32)
    v = v.reshape(b_sz, tl, GLA_HEADS, GLA_HEAD_V).astype(jnp.float32)
    gk = gk.reshape(b_sz, tl, GLA_HEADS, GLA_HEAD_K)
    o = gla_chunk_scan(q, k, v, gk)
    o = rmsnorm(o, o_norm_w).reshape(b_sz, tl, GLA_VAL_DIM)
    o = o * jax.nn.silu(g.astype(jnp.float32))
    return o.astype(u.dtype) @ w_out


def causal_depthwise_conv(u, w, b):
    out = lax.conv_general_dilated(
        u, w[:, None, :].astype(u.dtype), window_strides=(1,),
        padding=[(SSD_CONV - 1, 0)], dimension_numbers=('NWC', 'WIO', 'NWC'),
        feature_group_count=u.shape[-1])
    return out + b


def ssd_chunk_scan(xs, dt, A, Bm, Cm):
    b_sz = xs.shape[0]
    causal = jnp.tril(jnp.ones((CHUNK, CHUNK), dtype=bool))

    def step(S, inp):
        xc, dtc, bc, cc = inp
        a = jnp.cumsum(dtc * A, axis=1)
        a_last = a[:, -1]
        diff = a[:, :, None] - a[:, None]
        L = jnp.exp(jnp.where(causal[None, :, :, None, None], diff, -jnp.inf))
        cb = jnp.einsum('bign,bjgn->bijg', cc, bc)
        scores = cb[..., None] * L * dtc[:, None]
        y_intra = jnp.einsum('bijgh,bjghp->bighp', scores, xc)
        y_inter = jnp.einsum('bign,bghpn->bighp', cc, S) * jnp.exp(a)[..., None]
        w_state = jnp.exp(a_last[:, None] - a) * dtc
        S = (jnp.exp(a_last)[..., None, None] * S
             + jnp.einsum('bjgh,bjghp,bjgn->bghpn', w_state, xc, bc))
        return S, y_intra + y_inter

    S0 = jnp.zeros((b_sz, SSD_GROUPS, SSD_HEADS_PER_GROUP, SSD_HEAD_DIM, SSD_STATE), jnp.float32)
    _, y = lax.scan(step, S0, (to_chunks(xs), to_chunks(dt), to_chunks(Bm), to_chunks(Cm)))
    return from_chunks(y)


def ssd_mixer(u, w_in, conv_w, conv_b, dt_bias, a_log, d_skip, gnorm_w, w_out):
    b_sz, tl, _ = u.shape
    proj = u @ w_in
    z, xbc, dt = jnp.split(proj, [SSD_INNER, SSD_INNER + SSD_CONV_DIM], axis=-1)
    xbc = jax.nn.silu(causal_depthwise_conv(xbc, conv_w, conv_b))
    xs, Bm, Cm = jnp.split(xbc, [SSD_INNER, SSD_INNER + SSD_GROUPS * SSD_STATE], axis=-1)
    xs = xs.reshape(b_sz, tl, SSD_GROUPS, SSD_HEADS_PER_GROUP, SSD_HEAD_DIM).astype(jnp.float32)
    Bm = Bm.reshape(b_sz, tl, SSD_GROUPS, SSD_STATE).astype(jnp.float32)
    Cm = Cm.reshape(b_sz, tl, SSD_GROUPS, SSD_STATE).astype(jnp.float32)
    dt = jax.nn.softplus(dt.astype(jnp.float32) + dt_bias.astype(jnp.float32))
    dt = dt.reshape(b_sz, tl, SSD_GROUPS, SSD_HEADS_PER_GROUP)
    A = -jnp.exp(a_log.astype(jnp.float32)).reshape(SSD_GROUPS, SSD_HEADS_PER_GROUP)
    y = ssd_chunk_scan(xs, dt, A, Bm, Cm)
    y = y + d_skip.astype(jnp.float32).reshape(SSD_GROUPS, SSD_HEADS_PER_GROUP)[..., None] * xs
    y = y.reshape(b_sz, tl, SSD_INNER) * jax.nn.silu(z.astype(jnp.float32))
    y = rmsnorm(y.reshape(b_sz, tl, SSD_GROUPS, SSD_INNER // SSD_GROUPS),
                gnorm_w.reshape(SSD_GROUPS, SSD_INNER // SSD_GROUPS))
    return y.reshape(b_sz, tl, SSD_INNER).astype(u.dtype) @ w_out


def sq_relu_mlp(u, w_fc1, w_fc2):
    hdn = jax.nn.relu(u @ w_fc1)
    return (hdn * hdn) @ w_fc2


def setup_inputs(seed: int = 0) -> dict:
    key = jax.random.key(seed)
    ks = jax.random.split(key, 24)
    n_gla = (DEPTH + 1) // 2
    n_ssd = DEPTH // 2
    f32 = jnp.float32

    def nrm(k, shape, scale):
        return jax.random.normal(k, shape, f32) * scale

    def gain(k, shape):
        return 1.0 + 0.02 * jax.random.normal(k, shape, f32)

    x = jax.random.normal(ks[0], (BATCH, SEQ, D_MODEL), f32)
    mixer_norm_w = gain(ks[1], (DEPTH, D_MODEL))
    gla_w_in = nrm(ks[2], (n_gla, D_MODEL, GLA_IN), D_MODEL ** -0.5)
    gla_w_gk_up = nrm(ks[3], (n_gla, GLA_GATE_RANK, GLA_KEY_DIM), GLA_GATE_RANK ** -0.5)
    gla_b_gk_up = 2.0 + 0.1 * jax.random.normal(ks[4], (n_gla, GLA_KEY_DIM), f32)
    gla_o_norm_w = gain(ks[5], (n_gla, GLA_HEAD_V))
    gla_w_out = nrm(ks[6], (n_gla, GLA_VAL_DIM, D_MODEL), GLA_VAL_DIM ** -0.5)
    ssd_w_in = nrm(ks[7], (n_ssd, D_MODEL, SSD_IN), D_MODEL ** -0.5)
    ssd_conv_w = nrm(ks[8], (n_ssd, SSD_CONV, SSD_CONV_DIM), SSD_CONV ** -0.5)
    ssd_conv_b = nrm(ks[9], (n_ssd, SSD_CONV_DIM), 0.02)
    dt0 = jnp.exp(jax.random.uniform(ks[10], (n_ssd, SSD_HEADS), f32,
                                     np.log(1e-3).astype(np.float32), np.log(1e-1).astype(np.float32)))
    ssd_dt_bias = dt0 + jnp.log(-jnp.expm1(-dt0))
    ssd_a_log = jnp.log(jax.random.uniform(ks[11], (n_ssd, SSD_HEADS), f32, 1.0, 16.0))
    ssd_d_skip = gain(ks[12], (n_ssd, SSD_HEADS))
    ssd_gnorm_w = gain(ks[13], (n_ssd, SSD_INNER))
    ssd_w_out = nrm(ks[14], (n_ssd, SSD_INNER, D_MODEL), SSD_INNER ** -0.5)
    mlp_norm_w = gain(ks[15], (DEPTH, D_MODEL))
    mlp_w_fc1 = nrm(ks[16], (DEPTH, D_MODEL, D_FF), D_MODEL ** -0.5)
    mlp_w_fc2 = nrm(ks[17], (DEPTH, D_FF, D_MODEL), D_FF ** -0.5)
    final_norm_w = gain(ks[18], (D_MODEL,))
    return {"x": x, "mixer_norm_w": mixer_norm_w,
            "gla_w_in": gla_w_in, "gla_w_gk_up": gla_w_gk_up, "gla_b_gk_up": gla_b_gk_up,
            "gla_o_norm_w": gla_o_norm_w, "gla_w_out": gla_w_out,
            "ssd_w_in": ssd_w_in, "ssd_conv_w": ssd_conv_w, "ssd_conv_b": ssd_conv_b,
            "ssd_dt_bias": ssd_dt_bias, "ssd_a_log": ssd_a_log, "ssd_d_skip": ssd_d_skip,
            "ssd_gnorm_w": ssd_gnorm_w, "ssd_w_out": ssd_w_out,
            "mlp_norm_w": mlp_norm_w, "mlp_w_fc1": mlp_w_fc1, "mlp_w_fc2": mlp_w_fc2,
            "final_norm_w": final_norm_w}


def reference(x, mixer_norm_w, gla_w_in, gla_w_gk_up, gla_b_gk_up, gla_o_norm_w, gla_w_out,
              ssd_w_in, ssd_conv_w, ssd_conv_b, ssd_dt_bias, ssd_a_log, ssd_d_skip,
              ssd_gnorm_w, ssd_w_out, mlp_norm_w, mlp_w_fc1, mlp_w_fc2, final_norm_w):
    h = x
    for i in range(DEPTH):
        j = i // N_MIXERS
        u = rmsnorm(h, mixer_norm_w[i])
        if i % N_MIXERS == 0:
            h = h + gla_mixer(u, gla_w_in[j], gla_w_gk_up[j], gla_b_gk_up[j],
                              gla_o_norm_w[j], gla_w_out[j])
        else:
            h = h + ssd_mixer(u, ssd_w_in[j], ssd_conv_w[j], ssd_conv_b[j], ssd_dt_bias[j],
                              ssd_a_log[j], ssd_d_skip[j], ssd_gnorm_w[j], ssd_w_out[j])
        u = rmsnorm(h, mlp_norm_w[i])
        h = h + sq_relu_mlp(u, mlp_w_fc1[i], mlp_w_fc2[i])
    return rmsnorm(h, final_norm_w)
```

```python
import numpy as np
import os
os_environ_get = os.environ.get
from contextlib import ExitStack
import concourse.bass as bass
import concourse.mybir as mybir
from concourse.bass_utils import run_bass_kernel_spmd

F32 = mybir.dt.float32
BF16 = mybir.dt.bfloat16
ALU = mybir.AluOpType
AF = mybir.ActivationFunctionType

D = 2048
NCORES = 8
SEQ = 4096
BATCH = 4
HALF = SEQ // 2
EPS = 1e-5


class Buf:
    __slots__ = ("name", "w", "r", "excl")

    def __init__(self, name, excl=False):
        self.name = name
        self.w = None
        self.r = {}
        self.excl = excl


class Sched:
    def __init__(self, nc, es, same_eng_sync=True):
        self.nc = nc
        self.es = es
        self.engs = {"pe": nc.tensor, "dve": nc.vector, "act": nc.scalar,
                     "pool": nc.gpsimd, "sp": nc.sync}
        self.sems = {}
        self.cnt = {}
        for k in self.engs:
            self.sems[k] = es.enter_context(nc.semaphore("sem_" + k))
            self.cnt[k] = 0
        self.waited = {k: {} for k in self.engs}
        self.same = same_eng_sync
        self.nops = 0

    def new_dma_sem(self, name):
        key = "dma_" + name
        self.sems[key] = self.es.enter_context(self.nc.semaphore(key))
        self.cnt[key] = 0
        return key

    def _deps(self, reads, writes, e=None):
        deps = {}
        for b in reads:
            if b.w is not None:
                k, v = b.w
                if deps.get(k, 0) < v:
                    deps[k] = v
            if b.excl:
                for k, v in b.r.items():
                    if k != e and deps.get(k, 0) < v:
                        deps[k] = v
        for b in writes:
            if b.w is not None:
                k, v = b.w
                if deps.get(k, 0) < v:
                    deps[k] = v
            for k, v in b.r.items():
                if deps.get(k, 0) < v:
                    deps[k] = v
        return deps

    def _wait(self, e, deps):
        for k, v in deps.items():
            if k == e and (not self.same or e == "pe"):
                continue
            if self.waited[e].get(k, 0) >= v:
                continue
            self.engs[e].wait_ge(self.sems[k], v)
            self.waited[e][k] = v

    def op(self, e, fn, reads=(), writes=(), signal=True):
        self._wait(e, self._deps(reads, writes, e))
        ins = fn(self.engs[e])
        self.nops += 1
        if signal:
            self.cnt[e] += 1
            ins.then_inc(self.sems[e], 1)
            v = self.cnt[e]
        else:
            v = self.cnt[e] + 1
        for b in reads:
            if b.r.get(e, 0) < v:
                b.r[e] = v
        for b in writes:
            b.w = (e, v)
            b.r = {}
        return ins

    def dma(self, q, dkey, out_ap, in_ap, reads=(), writes=(), **kw):
        self._wait(q, self._deps(reads, writes))
        ins = self.engs[q].dma_start(out=out_ap, in_=in_ap, **kw)
        self.cnt[dkey] += 16
        ins.then_inc(self.sems[dkey], 16)
        v = self.cnt[dkey]
        for b in reads:
            b.r[dkey] = v
        for b in writes:
            b.w = (dkey, v)
            b.r = {}
        return ins

    def wait_all(self, e, keys):
        for k in keys:
            v = self.cnt[k]
            if v > 0 and self.waited[e].get(k, 0) < v:
                self.engs[e].wait_ge(self.sems[k], v)
                self.waited[e][k] = v


def tile_specs(layers):
    sp = []
    ar = np.arange

    def mlp(i):
        for kb in range(4):
            for j in range(4):
                c0 = (4 * kb + j) * 512
                sp.append(("fc1", "mlp_w_fc1", i, 0, ar(c0, c0 + 512)))
            for nb in range(4):
                sp.append(("fc2", "mlp_w_fc2", i, kb * 2048, ar(nb * 512, nb * 512 + 512)))

    for L in layers:
        if L == 0:
            sp.append(("gr", "gla_w_in", 0, 0, ar(6144, 6160)))
            for h in range(4):
                sp.append(("qk", "gla_w_in", 0, 0,
                           np.concatenate([ar(h * 256, h * 256 + 256), ar(1024 + h * 256, 1024 + h * 256 + 256)])))
                sp.append(("v", "gla_w_in", 0, 0, ar(2048 + h * 512, 2048 + h * 512 + 512)))
                sp.append(("g", "gla_w_in", 0, 0, ar(4096 + h * 512, 4096 + h * 512 + 512)))
            for nb in range(4):
                sp.append(("gout", "gla_w_out", 0, 0, ar(nb * 512, nb * 512 + 512)))
            mlp(0)
        else:
            sp.append(("dt", "ssd_w_in", 0, 0, ar(10240, 10304)))
            for gp in range(4):
                g0, g1 = 2 * gp, 2 * gp + 1
                sp.append(("bc", "ssd_w_in", 0, 0, np.concatenate([
                    ar(8192 + g0 * 128, 8192 + g0 * 128 + 128), ar(9216 + g0 * 128, 9216 + g0 * 128 + 128),
                    ar(8192 + g1 * 128, 8192 + g1 * 128 + 128), ar(9216 + g1 * 128, 9216 + g1 * 128 + 128)])))
                for g in (g0, g1):
                    sp.append(("xs", "ssd_w_in", 0, 0, ar(4096 + g * 512, 4096 + g * 512 + 512)))
                    sp.append(("z", "ssd_w_in", 0, 0, ar(g * 512, g * 512 + 512)))
            for kb in range(2):
                for nb in range(4):
                    sp.append(("sout", "ssd_w_out", 0, kb * 2048, ar(nb * 512, nb * 512 + 512)))
            mlp(1)
    return sp


def build_wstream(inputs, layers):
    parts = []
    for (_, key, slot, r0, cols) in tile_specs(layers):
        w = inputs[key][slot]
        blk = w[r0:r0 + 2048][:, cols]
        parts.append(np.ascontiguousarray(
            blk.reshape(16, 128, len(cols)).transpose(1, 0, 2)).reshape(-1))
    return np.concatenate(parts).astype(np.float32, copy=False)


def build_program(layers, n_pre, n_main, TT, final_norm, wtotal, nwb=3):
    NS = TT // 128
    NT = n_pre + n_main
    nc = bass.Bass("TRN2", target_bir_lowering=False)
    specs = tile_specs(layers)
    x_d = nc.dram_tensor("x", [NT * TT, D], F32, kind="ExternalInput").ap()
    w_d = nc.dram_tensor("wstream", [wtotal], F32, kind="ExternalInput").ap()
    normw_d = nc.dram_tensor("normw", [128, 4 * 16], F32, kind="ExternalInput").ap()
    fnw_d = nc.dram_tensor("fnw", [1, D], F32, kind="ExternalInput").ap()
    onw_d = nc.dram_tensor("onw", [128, 4], F32, kind="ExternalInput").ap()
    gnw_d = nc.dram_tensor("gnw", [128, 32], F32, kind="ExternalInput").ap()
    bgk_d = nc.dram_tensor("bgk", [1, 1024], F32, kind="ExternalInput").ap()
    wgk_d = nc.dram_tensor("wgk", [16, 1024], F32, kind="ExternalInput").ap()
    hv_d = nc.dram_tensor("headvec", [1, 3 * 64], F32, kind="ExternalInput").ap()
    cw_d = nc.dram_tensor("convw", [128, 48 * 4], F32, kind="ExternalInput").ap()
    cb_d = nc.dram_tensor("convb", [128, 48], F32, kind="ExternalInput").ap()
    flag_d = nc.dram_tensor("flag", [128, 1], F32, kind="ExternalInput").ap()
    out_d = nc.dram_tensor("out", [n_main * TT, D], F32, kind="ExternalOutput").ap()
    wbf_d = nc.dram_tensor("wbf16", [wtotal], BF16, kind="Internal").ap()

    es = ExitStack()
    with es:
        es.enter_context(nc.allow_low_precision("bf16 matmul operands, fp32 accumulation"))
        S = Sched(nc, es)

        def sb(name, shape, dt):
            return es.enter_context(nc.sbuf_tensor("sb_" + name, shape, dt)), Buf(name)

        hres, b_hres = sb("hres", [128, NS, D], F32)
        uT, b_uT = sb("uT", [128, 16, TT], BF16)
        wbufs = [sb(f"wb{i}", [128, 16, 512], BF16) for i in range(nwb)]
        actT = [sb(f"actT{i}", [128, 16, TT], BF16) for i in range(2)]
        tm_a, b_tma = sb("tm_a", [128, NS, 512], F32)
        tm_b, b_tmb = sb("tm_b", [128, NS, 512], F32)
        ubf, b_ubf = sb("ubf", [128, NS, D], BF16)
        b_ubfp = [[Buf(f"ubfp{j}_{i}") for i in range(4)] for j in range(NS)]
        nss, _ = sb("nss", [128, NS], F32)
        b_nss = [Buf(f"nss{i}") for i in range(NS)]
        ss, b_ss = sb("ss", [128, 8], F32)
        ident, b_id = sb("ident", [128, 128], BF16)
        tri, b_tri = sb("tri", [128, 128], F32)
        triU, b_triU = sb("triU", [128, 128], F32)
        ones, b_ones = sb("ones", [128, 128], F32)
        normw, b_normw = sb("normw", [128, 4, 16], F32)
        fnw, b_fnw = sb("fnw", [128, D], F32)
        onw, b_onw = sb("onw", [128, 4], F32)
        gnw, b_gnw = sb("gnw", [128, 32], F32)
        bgk, b_bgk = sb("bgk", [128, 1024], F32)
        wgk, b_wgk = sb("wgk", [16, 1024], BF16)
        hv, b_hv = sb("hv", [128, 3, 64], F32)
        negA, b_negA = sb("negA", [128, 64], F32)
        cw, b_cw = sb("cw", [128, 48, 4], F32)
        cb, b_cb = sb("cb", [128, 48], F32)
        flag, b_flag = sb("flag", [128, 1], F32)
        Sg, b_Sg = sb("Sg", [128, 4, 2, 512], F32)
        St, b_St = sb("St", [128, 8, 512], F32)
        cs, b_cs = sb("cs", [128, 48, 3], F32)
        onb, b_onb = sb("onb", [128, NS, 512], BF16)
        rtmp, b_rtmp = sb("rtmp", [128, TT], F32)
        phase_ctr = [0]
        prev_phase_bufs = []

        def phase_alloc(pes, decls):
            phase_ctr[0] += 1
            out = {}
            for name, shape, dt in decls:
                t = pes.enter_context(nc.sbuf_tensor(f"ph{phase_ctr[0]}_{name}", shape, dt))
                out[name] = (t, Buf(name))
            for e in ("pe", "dve", "act", "pool"):
                S._wait(e, S._deps([], prev_phase_bufs, e))
            prev_phase_bufs[:] = [bb for (_, bb) in out.values()]
            return out

        NPS = 6
        psb = []
        for i in range(NPS):
            t = es.enter_context(nc.psum_tensor(f"ps{i}", [128, 512], F32))
            psb.append((t, Buf(f"ps{i}", excl=True)))
        pT = []
        for i in range(2):
            t = es.enter_context(nc.psum_tensor(f"psT{i}", [128, 8, 128], BF16))
            pT.append((t, 0, Buf(f"psT{i}", excl=True)))
        ps_i = [0]
        pt_i = [0]

        def psum():
            t, b = psb[ps_i[0] % NPS]
            ps_i[0] += 1
            return t, b

        def psumT():
            r = pT[pt_i[0] % 2]
            pt_i[0] += 1
            return r

        d_const = S.new_dma_sem("const")
        d_x = S.new_dma_sem("x")
        d_out = S.new_dma_sem("out")
        d_w = [S.new_dma_sem(f"w{i}") for i in range(nwb)]

        b_const = Buf("consts")
        cl = [(normw[:].rearrange("p a b -> p (a b)"), normw_d), (fnw[:], fnw_d.to_broadcast([128, D])),
              (onw[:], onw_d), (gnw[:], gnw_d), (bgk[:], bgk_d.to_broadcast([128, 1024])),
              (hv[:].rearrange("p a b -> p (a b)"), hv_d.to_broadcast([128, 192])),
              (cw[:].rearrange("p a b -> p (a b)"), cw_d), (cb[:], cb_d), (flag[:], flag_d)]
        for o_ap, i_ap in cl:
            S.dma("sp", d_const, o_ap, i_ap, writes=[b_const])
        b_wgk2 = Buf("wgk2")
        S.dma("pool", S.new_dma_sem("wgk"), wgk[:], wgk_d, writes=[b_wgk2])
        S.op("pool", lambda e: e.memset(ident[:], 1.0), writes=[b_id])
        S.op("pool", lambda e: e.affine_select(out=ident[:], in_=ident[:], pattern=[[-1, 128]], compare_op=ALU.is_equal,
                                               fill=0.0, base=0, channel_multiplier=1), reads=[b_id], writes=[b_id])
        S.op("pool", lambda e: e.memset(tri[:], 1.0), writes=[b_tri])
        S.op("pool", lambda e: e.affine_select(out=tri[:], in_=tri[:], pattern=[[1, 128]], compare_op=ALU.is_ge,
                                               fill=0.0, base=0, channel_multiplier=-1), reads=[b_tri], writes=[b_tri])
        S.op("pool", lambda e: e.memset(triU[:], 1.0), writes=[b_triU])
        S.op("pool", lambda e: e.affine_select(out=triU[:], in_=triU[:], pattern=[[-1, 128]], compare_op=ALU.is_gt,
                                               fill=0.0, base=0, channel_multiplier=1), reads=[b_triU], writes=[b_triU])
        S.op("pool", lambda e: e.memset(ones[:], 1.0), writes=[b_ones])
        S.op("pool", lambda e: e.memset(Sg[:], 0.0), writes=[b_Sg])
        S.op("pool", lambda e: e.memset(St[:], 0.0), writes=[b_St])
        S.op("pool", lambda e: e.memset(cs[:], 0.0), writes=[b_cs])
        S.op("act", lambda e: e.activation(out=negA[:], in_=hv[:, 1, :], func=AF.Exp), reads=[b_const], writes=[b_negA])
        S.op("dve", lambda e: e.tensor_scalar(out=negA[:], in0=negA[:], scalar1=-1.0, scalar2=None, op0=ALU.mult),
             reads=[b_negA], writes=[b_negA])
        CONST = [b_const, b_id, b_tri, b_triU, b_ones, b_negA]

        offs = []
        o = 0
        for sp_ in specs:
            offs.append(o)
            o += 2048 * len(sp_[4])
        assert o == wtotal, (o, wtotal)
        def skip_in_prefix(sp_):
            return (sp_[0] in ("z", "sout")) or (sp_[0] in ("fc1", "fc2") and sp_[2] == 1)
        seq = []
        for t in range(NT):
            for i, sp_ in enumerate(specs):
                if t < n_pre and skip_in_prefix(sp_):
                    continue
                seq.append(i)
        wstate = {"issued": 0, "next": 0, "total": len(seq)}
        d_ws = [S.new_dma_sem(f"ws{i}") for i in range(nwb)]
        b_scr = [Buf(f"scr{i}") for i in range(len(specs))]
        converted = [False] * len(specs)

        def issue_w(n):
            t = seq[n]
            ncols = len(specs[t][4])
            wt, bw = wbufs[n % nwb]
            if not converted[t]:
                src = w_d[offs[t]: offs[t] + 2048 * ncols].rearrange("(p k n) -> p k n", p=128, k=16)
                S.dma("pool", d_w[n % nwb], wt[:, :, 0:ncols], src, writes=[bw])
                if NT > 1:
                    dst = wbf_d[offs[t]: offs[t] + 2048 * ncols].rearrange("(p k n) -> p k n", p=128, k=16)
                    S.dma("sp", d_ws[n % nwb], dst, wt[:, :, 0:ncols], reads=[bw], writes=[b_scr[t]])
                converted[t] = True
            else:
                src = wbf_d[offs[t]: offs[t] + 2048 * ncols].rearrange("(p k n) -> p k n", p=128, k=16)
                S.dma("sp", d_wl[n % nwb], wt[:, :, 0:ncols], src, reads=[b_scr[t]], writes=[bw])

        d_wl = [S.new_dma_sem(f"wl{i}") for i in range(nwb)]

        def next_tile(kind):
            n = wstate["next"]
            assert specs[seq[n]][0] == kind, (specs[seq[n]][0], kind)
            while wstate["issued"] < min(n + nwb, wstate["total"]):
                issue_w(wstate["issued"])
                wstate["issued"] += 1
            wstate["next"] += 1
            return wbufs[n % nwb]

        def rstd_from_ss(n):
            pass

        def norm_to_uT(widx):
            for s in range(NS):
                S.op("act", lambda e: e.activation(out=ubf[:, s, :], in_=hres[:, s, :], func=AF.Square,
                                                   accum_out=nss[:, s:s + 1]),
                     reads=[b_hres], writes=b_ubfp[s] + [b_nss[s]])
                S.op("act", lambda e: e.activation(out=nss[:, s:s + 1], in_=nss[:, s:s + 1], func=AF.Ln, scale=1.0 / D,
                                                   bias=EPS), reads=[b_nss[s]], writes=[b_nss[s]])
                S.op("act", lambda e: e.activation(out=nss[:, s:s + 1], in_=nss[:, s:s + 1], func=AF.Exp, scale=-0.5),
                     reads=[b_nss[s]], writes=[b_nss[s]])
            for s in range(NS):
                for grp in range(4):
                    S.op("dve",
                         lambda e: e.tensor_scalar(out=ubf[:, s, grp * 512:(grp + 1) * 512],
                                                   in0=hres[:, s, grp * 512:(grp + 1) * 512],
                                                   scalar1=nss[:, s:s + 1], scalar2=None, op0=ALU.mult),
                         reads=[b_hres, b_nss[s]], writes=[b_ubfp[s][grp]])
            k = 0
            for s in range(NS):
                for grp in range(4):
                    pt, po, bpt = psumT()
                    for j in range(4):
                        kc = grp * 4 + j
                        S.op("pe", lambda e: e.transpose(out=pt[:, po + j, :], in_=ubf[:, s, kc * 128:(kc + 1) * 128],
                                                         identity=ident[:]),
                             reads=[b_ubfp[s][grp], b_id], writes=[bpt], signal=(j == 3))
                    if k % 2 == 0:
                        S.op("dve", lambda e: e.tensor_tensor(
                            out=uT[:, grp * 4:(grp + 1) * 4, s * 128:(s + 1) * 128], in0=pt[:, po:po + 4, :],
                            in1=normw[:, widx, grp * 4:(grp + 1) * 4].unsqueeze(2).to_broadcast([128, 4, 128]),
                            op=ALU.mult), reads=[bpt, b_const], writes=[b_uT])
                    else:
                        for j in range(4):
                            kc = grp * 4 + j
                            S.op("act", lambda e: e.mul(out=uT[:, kc, s * 128:(s + 1) * 128], in_=pt[:, po + j, :],
                                                        mul=normw[:, widx, kc:kc + 1]),
                                 reads=[bpt, b_const], writes=[b_uT])
                    k += 1

        def proj_fm(wt, bw, c0, ncol, evac):
            ps, bps = psum()
            for kc in range(16):
                S.op("pe", lambda e: e.matmul(ps[0:ncol, 0:TT], lhsT=wt[:, kc, c0:c0 + ncol], rhs=uT[:, kc, :],
                                              start=(kc == 0), stop=(kc == 15)),
                     reads=[bw, b_uT], writes=[bps], signal=(kc == 15))
            evac(ps, bps)

        def proj_tm(wt, bw, src, bsrc, s, c0, ncol, evac):
            ps, bps = psum()
            for kc in range(16):
                S.op("pe", lambda e: e.matmul(ps[:, 0:ncol], lhsT=src[:, kc, s * 128:(s + 1) * 128],
                                              rhs=wt[:, kc, c0:c0 + ncol], start=(kc == 0), stop=(kc == 15)),
                     reads=[bw, bsrc], writes=[bps], signal=(kc == 15))
            evac(ps, bps)

        def outproj(kind, src, bsrc, nb):
            wt, bw = next_tile(kind)
            for s in range(NS):
                def ev(ps, bps, s=s):
                    S.op("dve", lambda e: e.tensor_tensor(out=hres[:, s, nb * 512:(nb + 1) * 512],
                                                          in0=hres[:, s, nb * 512:(nb + 1) * 512], in1=ps[:, :],
                                                          op=ALU.add),
                         reads=[bps, b_hres], writes=[b_hres])
                proj_tm(wt, bw, src, bsrc, s, 0, 512, ev)

        def mlp():
            for kb in range(4):
                at, bat = actT[kb % 2]
                for j in range(4):
                    wt, bw = next_tile("fc1")
                    for fb in range(4):
                        def ev(ps, bps, j=j, fb=fb):
                            S.op("act", lambda e: e.activation(out=rtmp[:], in_=ps[:, 0:TT], func=AF.Relu),
                                 reads=[bps], writes=[b_rtmp])
                            S.op("dve", lambda e: e.tensor_tensor(out=at[:, j * 4 + fb, :], in0=rtmp[:], in1=rtmp[:],
                                                                  op=ALU.mult),
                                 reads=[b_rtmp], writes=[bat])
                        proj_fm(wt, bw, fb * 128, 128, ev)
                for nb in range(4):
                    outproj("fc2", at, bat, nb)

        def transpose_to_actT(srcb, bsrc, s, dst, bdst, kc0, scale_ap):
            pt, po, bpt = psumT()
            for ec in range(4):
                S.op("pe", lambda e: e.transpose(out=pt[:, po + ec, :], in_=srcb[:, s, ec * 128:(ec + 1) * 128],
                                                 identity=ident[:]),
                     reads=[bsrc, b_id], writes=[bpt], signal=(ec == 3))
            S.op("dve", lambda e: e.tensor_tensor(
                out=dst[:, kc0:kc0 + 4, s * 128:(s + 1) * 128], in0=pt[:, po:po + 4, :],
                in1=scale_ap.unsqueeze(2).to_broadcast([128, 4, 128]), op=ALU.mult),
                reads=[bpt, b_const], writes=[bdst])

        def gla():
            aT, baT = actT[0]
            pes = ExitStack()
            dec = [("lsp0", [128, 1024], F32), ("lsp1", [128, 1024], F32), ("grT", [16, TT], BF16),
                   ("qT", [128, 2, TT], F32), ("kT", [128, 2, TT], F32), ("ktok", [128, NS, 256], F32),
                   ("vtok", [128, NS, 512], BF16), ("Sb", [128, 2, 512], BF16), ("tpre", [128, 512], F32),
                   ("junk", [128, 512], BF16)]
            for i in range(NS):
                dec += [(f"Eq{i}", [128, 2, 128], F32), (f"Ek{i}", [128, 2, 128], F32), (f"Ed{i}", [128, 256], F32),
                        (f"qtl{i}", [128, 2, 128], BF16), (f"ktl{i}", [128, 2, 128], BF16),
                        (f"kdec{i}", [128, 256], BF16), (f"attT{i}", [128, 128], BF16)]
            T = phase_alloc(pes, dec)
            lsps = [T["lsp0"], T["lsp1"]]
            grT, b_grT = T["grT"]
            qT, b_qT = T["qT"]
            kT, b_kT = T["kT"]
            ktok, b_ktok = T["ktok"]
            vtok, b_vtok = T["vtok"]
            Sb, b_Sb = T["Sb"]
            tpre, b_tpre = T["tpre"]
            pjunk, b_pjunk = T["junk"]
            wt, bw = next_tile("gr")

            def ev_gr(ps, bps):
                S.op("act", lambda e: e.copy(out=grT[:], in_=ps[0:16, 0:TT]), reads=[bps], writes=[b_grT])
            proj_fm(wt, bw, 0, 16, ev_gr)
            ck("gr")
            for s in range(NS):
                for hf in range(2):
                    ps, bps = psum()
                    S.op("pe", lambda e: e.matmul(ps[:, :], lhsT=grT[0:16, s * 128:(s + 1) * 128],
                                                  rhs=wgk[0:16, hf * 512:(hf + 1) * 512], start=True, stop=True),
                         reads=[b_grT, b_wgk2], writes=[bps])
                    S.op("dve", lambda e: e.tensor_tensor(out=tpre[:], in0=ps[:, :], in1=bgk[:, hf * 512:(hf + 1) * 512],
                                                          op=ALU.add), reads=[bps, b_const], writes=[b_tpre])
                    S.op("act", lambda e: e.activation(out=tpre[:], in_=tpre[:], func=AF.Exp, scale=-1.0),
                         reads=[b_tpre], writes=[b_tpre])
                    S.op("act", lambda e: e.activation(out=lsps[s][0][:, hf * 512:(hf + 1) * 512], in_=tpre[:], func=AF.Ln,
                                                       bias=1.0), reads=[b_tpre], writes=[lsps[s][1]])
            ck("lsp")
            def g_qkv(h):
                wt, bw = next_tile("qk")
                for fb in range(4):
                    dstT, bd = (qT, b_qT) if fb < 2 else (kT, b_kT)

                    def ev(ps, bps, fb=fb, dstT=dstT, bd=bd):
                        S.op("act", lambda e: e.copy(out=dstT[:, fb % 2, :], in_=ps[:, 0:TT]), reads=[bps], writes=[bd])
                    proj_fm(wt, bw, fb * 128, 128, ev)
                for s in range(NS):
                    def ev(ps, bps, s=s):
                        S.op("act", lambda e: e.copy(out=ktok[:, s, :], in_=ps[:, 0:256]), reads=[bps], writes=[b_ktok])
                    proj_tm(wt, bw, uT, b_uT, s, 256, 256, ev)
                ck("qk")
                wt, bw = next_tile("v")
                for s in range(NS):
                    def ev(ps, bps, s=s):
                        S.op("act", lambda e: e.copy(out=vtok[:, s, :], in_=ps[:, :]), reads=[bps], writes=[b_vtok])
                    proj_tm(wt, bw, uT, b_uT, s, 0, 512, ev)
            def g_g(h):
                wt, bw = next_tile("g")
                for s in range(NS):
                    def ev(ps, bps, s=s):
                        S.op("act", lambda e: e.activation(out=tm_b[:, s, :], in_=ps[:, :], func=AF.Silu),
                             reads=[bps], writes=[b_tmb])
                    proj_tm(wt, bw, uT, b_uT, s, 0, 512, ev)
            def g_scan(h):
                for s in range(NS):
                    Eq, b_Eq = T[f"Eq{s}"]
                    Ek, b_Ek = T[f"Ek{s}"]
                    Ed, b_Ed = T[f"Ed{s}"]
                    qtl, b_qtl = T[f"qtl{s}"]
                    ktl, b_ktl = T[f"ktl{s}"]
                    kdec, b_kdec = T[f"kdec{s}"]
                    attT, b_attT = T[f"attT{s}"]
                    sl = slice(s * 128, (s + 1) * 128)
                    lh = lsps[s][0][:, h * 256:(h + 1) * 256]
                    b_lsp_s = lsps[s][1]
                    pc, bpc = psum()
                    for dc in range(2):
                        S.op("pe", lambda e: e.matmul(pc[:, dc * 128:(dc + 1) * 128], lhsT=lh[:, dc * 128:(dc + 1) * 128],
                                                      rhs=tri[:], start=True, stop=True),
                             reads=[b_lsp_s, b_tri], writes=[bpc], signal=False)
                    S.op("pe", lambda e: e.matmul(pc[:, 256:512], lhsT=triU[:], rhs=lh, start=True, stop=True),
                         reads=[b_lsp_s, b_triU], writes=[bpc])
                    S.op("act", lambda e: e.activation(out=Eq[:].rearrange("p a b -> p (a b)"), in_=pc[:, 0:256],
                                                       func=AF.Exp, scale=-1.0 / 16), reads=[bpc], writes=[b_Eq])
                    S.op("act", lambda e: e.activation(out=Ek[:].rearrange("p a b -> p (a b)"), in_=pc[:, 0:256],
                                                       func=AF.Exp, scale=1.0 / 16), reads=[bpc], writes=[b_Ek])
                    S.op("act", lambda e: e.activation(out=Ed[:], in_=pc[:, 256:512], func=AF.Exp, scale=-1.0 / 16),
                         reads=[bpc], writes=[b_Ed])
                    ck("s1")
                    S.op("dve", lambda e: e.scalar_tensor_tensor(out=qtl[:], in0=qT[:, :, sl], scalar=1.0 / 16, in1=Eq[:],
                                                                 op0=ALU.mult, op1=ALU.mult),
                         reads=[b_qT, b_Eq], writes=[b_qtl])
                    S.op("dve", lambda e: e.tensor_tensor(out=ktl[:], in0=kT[:, :, sl], in1=Ek[:], op=ALU.mult),
                         reads=[b_kT, b_Ek], writes=[b_ktl])
                    S.op("dve", lambda e: e.tensor_tensor(out=kdec[:], in0=ktok[:, s, :], in1=Ed[:], op=ALU.mult),
                         reads=[b_ktok, b_Ed], writes=[b_kdec])
                    ck("s2")
                    pa, bpa = psum()
                    for dc in range(2):
                        S.op("pe", lambda e: e.matmul(pa[:, 0:128], lhsT=ktl[:, dc, :], rhs=qtl[:, dc, :],
                                                      start=(dc == 0), stop=(dc == 1)),
                             reads=[b_ktl, b_qtl], writes=[bpa], signal=(dc == 1))
                    S.op("dve", lambda e: e.tensor_tensor(out=attT[:], in0=pa[:, 0:128], in1=tri[:], op=ALU.mult),
                         reads=[bpa, b_tri], writes=[b_attT])
                for s in range(NS):
                    Eq, b_Eq = T[f"Eq{s}"]
                    Ek, b_Ek = T[f"Ek{s}"]
                    Ed, b_Ed = T[f"Ed{s}"]
                    qtl, b_qtl = T[f"qtl{s}"]
                    ktl, b_ktl = T[f"ktl{s}"]
                    kdec, b_kdec = T[f"kdec{s}"]
                    attT, b_attT = T[f"attT{s}"]
                    sl = slice(s * 128, (s + 1) * 128)
                    S.op("act", lambda e: e.copy(out=Sb[:], in_=Sg[:, h, :, :]), reads=[b_Sg], writes=[b_Sb])
                    ck("s3")
                    po_, bpo = psum()
                    S.op("pe", lambda e: e.matmul(po_[:, :], lhsT=attT[:], rhs=vtok[:, s, :], start=True, stop=False),
                         reads=[b_attT, b_vtok], writes=[bpo], signal=False)
                    for dc in range(2):
                        S.op("pe", lambda e: e.matmul(po_[:, :], lhsT=qtl[:, dc, :], rhs=Sb[:, dc, :], start=False,
                                                      stop=(dc == 1)),
                             reads=[b_qtl, b_Sb], writes=[bpo], signal=(dc == 1))
                    S.op("dve", lambda e: e.tensor_copy(out=tm_a[:, s, :], in_=po_[:, :]), reads=[bpo], writes=[b_tma])
                    S.op("act", lambda e: e.activation(out=pjunk[:], in_=po_[:, :], func=AF.Square,
                                                       accum_out=ss[:, s:s + 1]), reads=[bpo], writes=[b_pjunk, b_ss])
                    ck("s4")
                    for dc in range(2):
                        pd, bpd = psum()
                        S.op("pe", lambda e: e.matmul(pd[:, :], lhsT=kdec[:, dc * 128:(dc + 1) * 128], rhs=vtok[:, s, :],
                                                      start=True, stop=True), reads=[b_kdec, b_vtok], writes=[bpd])
                        S.op("dve", lambda e: e.scalar_tensor_tensor(out=Sg[:, h, dc, :], in0=Sg[:, h, dc, :],
                                                                     scalar=Eq[:, dc, 127:128], in1=pd[:, :],
                                                                     op0=ALU.mult, op1=ALU.add),
                             reads=[b_Sg, b_Eq, bpd], writes=[b_Sg])
            def g_post(h):
                S.op("act", lambda e: e.activation(out=ss[:, 0:NS], in_=ss[:, 0:NS], func=AF.Ln, scale=1.0 / 512, bias=EPS),
                     reads=[b_ss], writes=[b_ss])
                S.op("act", lambda e: e.activation(out=ss[:, 0:NS], in_=ss[:, 0:NS], func=AF.Exp, scale=-0.5),
                     reads=[b_ss], writes=[b_ss])
                for s in range(NS):
                    S.op("dve", lambda e: e.scalar_tensor_tensor(out=onb[:, s, :], in0=tm_a[:, s, :], scalar=ss[:, s:s + 1],
                                                                 in1=tm_b[:, s, :], op0=ALU.mult, op1=ALU.mult),
                         reads=[b_tma, b_tmb, b_ss], writes=[b_onb])

            def g_post_b(h):
                for s in range(NS):
                    transpose_to_actT(onb, b_onb, s, aT, baT, h * 4, onw[:, 0:4])
            g_qkv(0)
            g_g(0)
            for h in range(4):
                g_scan(h)
                g_post(h)
                if h < 3:
                    g_qkv(h + 1)
                    g_g(h + 1)
                g_post_b(h)
            ck("heads")
            pes.close()
            for nb in range(4):
                outproj("gout", aT, baT, nb)
            ck("gout")

        def ssd(state_only=False):
            so = state_only
            pes = ExitStack()
            dec = [("dtv", [128, NS, 64], F32), ("dtA", [128, NS, 64], F32), ("acum", [128, NS, 64], F32),
                   ("expa", [128, NS, 64], F32), ("nacum", [128, NS, 64], F32), ("pre", [128, 4, TT + 3], F32), ("cacc", [128, 4, TT], F32),
                   ("bcT", [128, 4, TT], BF16), ("xsT", [128, 4, TT], BF16), ("xstok", [128, NS, 512], BF16),
                   ("Btok", [128, NS, 128], BF16), ("Stb", [128, 512], BF16), ("ytmp", [128, 512], F32),
                   ("junk", [128, 512], BF16)]
            for i in range(NS):
                dec += [(f"Rm{i}", [128, 1024], F32), (f"dmat{i}", [128, 8, 128], F32), (f"scT{i}", [128, 8, 128], BF16),
                        (f"cbm{i}", [128, 128], F32), (f"xdt{i}", [128, 512], BF16), (f"xw{i}", [128, 512], BF16),
                        (f"wst{i}", [128, 8], F32), (f"ela{i}", [128, 8], F32), (f"xsd{i}", [128, 512], BF16)]
            T = phase_alloc(pes, dec)
            dtv, b_dtv = T["dtv"]
            dtA, b_dtA = T["dtA"]
            acum, b_acum = T["acum"]
            expa, b_expa = T["expa"]
            nacum, b_nacum = T["nacum"]
            pre, b_pre = T["pre"]
            cacc, b_cacc = T["cacc"]
            b_pres = [Buf(f"pre{i}") for i in range(4)]
            b_caccs = [Buf(f"cacc{i}") for i in range(4)]
            prev_phase_bufs.extend(b_pres + b_caccs)
            bcT, b_bcT = T["bcT"]
            xsT, b_xsT = T["xsT"]
            xstok, b_xstok = T["xstok"]
            Btok, b_Btok = T["Btok"]
            Stb, b_Stb = T["Stb"]
            ytmp, b_ytmp = T["ytmp"]
            pjunk, b_pjunk = T["junk"]

            def conv_tile(kind, blk0, dst, bdst):
                wt, bw = next_tile(kind)
                pend = [None]

                def conv_s(blk, slot):
                    S.op("act", lambda e: e.activation(out=dst[:, slot, :], in_=cacc[:, slot, :], func=AF.Silu,
                                                       bias=cb[:, blk:blk + 1]),
                         reads=[b_caccs[slot], b_const], writes=[bdst])

                for fb in range(4):
                    def ev(ps, bps, fb=fb):
                        blk, slot = blk0 + fb, fb
                        S.op("act", lambda e: e.copy(out=pre[:, slot, 0:3], in_=cs[:, blk, :]), reads=[b_cs],
                             writes=[b_pres[slot]])
                        S.op("act", lambda e: e.copy(out=pre[:, slot, 3:3 + TT], in_=ps[:, 0:TT]), reads=[bps],
                             writes=[b_pres[slot]])
                        S.op("act", lambda e: e.mul(out=cacc[:, slot, :], in_=ps[:, 0:TT], mul=cw[:, blk, 3:4]),
                             reads=[bps, b_const], writes=[b_caccs[slot]])
                        S.op("act", lambda e: e.copy(out=cs[:, blk, :], in_=pre[:, slot, TT:TT + 3]),
                             reads=[b_pres[slot]], writes=[b_cs])
                        if pend[0] is not None:
                            conv_s(*pend[0])
                        for k in range(3):
                            S.op("dve", lambda e: e.scalar_tensor_tensor(out=cacc[:, slot, :], in0=pre[:, slot, k:k + TT],
                                                                         scalar=cw[:, blk, k:k + 1], in1=cacc[:, slot, :],
                                                                         op0=ALU.mult, op1=ALU.add),
                                 reads=[b_pres[slot], b_caccs[slot], b_const], writes=[b_caccs[slot]])
                        pend[0] = (blk, slot)
                    proj_fm(wt, bw, fb * 128, 128, ev)
                conv_s(*pend[0])

            wt, bw = next_tile("dt")
            for s in range(NS):
                def ev(ps, bps, s=s):
                    S.op("dve", lambda e: e.tensor_tensor(out=dtv[:, s, :], in0=ps[:, 0:64], in1=hv[:, 0, :], op=ALU.add),
                         reads=[bps, b_const], writes=[b_dtv])
                proj_tm(wt, bw, uT, b_uT, s, 0, 64, ev)
            S.op("act", lambda e: e.activation(out=dtv[:], in_=dtv[:], func=AF.Exp), reads=[b_dtv], writes=[b_dtv])
            S.op("act", lambda e: e.activation(out=dtv[:], in_=dtv[:], func=AF.Ln, bias=1.0), reads=[b_dtv], writes=[b_dtv])
            for s in range(NS):
                S.op("dve", lambda e: e.tensor_tensor(out=dtA[:, s, :], in0=dtv[:, s, :], in1=negA[:], op=ALU.mult),
                     reads=[b_dtv, b_negA], writes=[b_dtA])
                ps, bps = psum()
                S.op("pe", lambda e: e.matmul(ps[:, 0:64], lhsT=tri[:], rhs=dtA[:, s, :], start=True, stop=True),
                     reads=[b_tri, b_dtA], writes=[bps])
                S.op("dve", lambda e: e.tensor_copy(out=acum[:, s, :], in_=ps[:, 0:64]), reads=[bps], writes=[b_acum])
                S.op("act", lambda e: e.activation(out=expa[:, s, :], in_=ps[:, 0:64], func=AF.Exp), reads=[bps],
                     writes=[b_expa])
            def s_proj(g):
                gp, gi = g // 2, g % 2
                if gi == 0:
                    conv_tile("bc", gp * 4, bcT, b_bcT)
                conv_tile("xs", 16 + g * 4, xsT, b_xsT)
                if not so:
                    wt, bw = next_tile("z")
                    for s in range(NS):
                        def ev(ps, bps, s=s):
                            S.op("act", lambda e: e.activation(out=tm_b[:, s, :], in_=ps[:, :], func=AF.Silu),
                                 reads=[bps], writes=[b_tmb])
                        proj_tm(wt, bw, uT, b_uT, s, 0, 512, ev)

            def s_tr(g):
                gi = g % 2
                for s in range(NS):
                    pt, po, bpt = psumT()
                    for fb in range(4):
                        S.op("pe", lambda e: e.transpose(out=pt[:, po + fb, :], in_=xsT[:, fb, s * 128:(s + 1) * 128],
                                                         identity=ident[:]),
                             reads=[b_xsT, b_id], writes=[bpt], signal=(fb == 3))
                    S.op("act", lambda e: e.copy(out=xstok[:, s, :].rearrange("p (a b) -> p a b", a=4),
                                                 in_=pt[:, po:po + 4, :]), reads=[bpt], writes=[b_xstok])
                    pt, po, bpt = psumT()
                    S.op("pe", lambda e: e.transpose(out=pt[:, po, :], in_=bcT[:, gi * 2, s * 128:(s + 1) * 128],
                                                     identity=ident[:]), reads=[b_bcT, b_id], writes=[bpt])
                    S.op("act", lambda e: e.copy(out=Btok[:, s, :], in_=pt[:, po, :]), reads=[bpt], writes=[b_Btok])

            def emit_Rm(g):
                hs = slice(g * 8, g * 8 + 8)
                for s in range(NS):
                    Rm, b_Rm = T[f"Rm{s}"]
                    S.op("dve", lambda e: e.tensor_tensor(
                        out=Rm[:, :].rearrange("p (a b) -> p a b", a=8),
                        in0=dtA[:, s, hs].unsqueeze(2).to_broadcast([128, 8, 128]),
                        in1=tri[:].unsqueeze(1).to_broadcast([128, 8, 128]), op=ALU.mult),
                        reads=[b_dtA, b_tri], writes=[b_Rm])

            def s_scan(g):
                if True:
                    gi = g % 2
                    hs = slice(g * 8, g * 8 + 8)
                    pas = {}
                    for s in range(NS):
                        Rm, b_Rm = T[f"Rm{s}"]
                        xdt, b_xdt = T[f"xdt{s}"]
                        xsd, b_xsd = T[f"xsd{s}"]
                        pa0, bpa0 = psum()
                        pa1, bpa1 = psum()
                        S.op("pe", lambda e: e.matmul(pa0[:, :], lhsT=ones[:], rhs=Rm[:, 0:512], start=True, stop=True),
                             reads=[b_ones, b_Rm], writes=[bpa0])
                        S.op("pe", lambda e: e.matmul(pa1[:, :], lhsT=ones[:], rhs=Rm[:, 512:1024], start=True, stop=True),
                             reads=[b_ones, b_Rm], writes=[bpa1])
                        pas[s] = ((pa0, bpa0), (pa1, bpa1))
                        S.op("dve", lambda e: e.tensor_tensor(
                            out=xdt[:].rearrange("p (a b) -> p a b", a=8),
                            in0=xstok[:, s, :].rearrange("p (a b) -> p a b", a=8),
                            in1=dtv[:, s, hs].unsqueeze(2).to_broadcast([128, 8, 64]), op=ALU.mult),
                            reads=[b_xstok, b_dtv], writes=[b_xdt])
                        if not so:
                            S.op("dve", lambda e: e.tensor_tensor(
                                out=xsd[:].rearrange("p (a b) -> p a b", a=8),
                                in0=xstok[:, s, :].rearrange("p (a b) -> p a b", a=8),
                                in1=hv[:, 2, hs].unsqueeze(2).to_broadcast([128, 8, 64]), op=ALU.mult),
                                reads=[b_xstok, b_const], writes=[b_xsd])
                    if g < 7:
                        emit_Rm(g + 1)
                    for s in range(NS):
                        dmat, b_dmat = T[f"dmat{s}"]
                        scT, b_scT = T[f"scT{s}"]
                        cbm, b_cbm = T[f"cbm{s}"]
                        xdt, b_xdt = T[f"xdt{s}"]
                        xw, b_xw = T[f"xw{s}"]
                        wst, b_wst = T[f"wst{s}"]
                        ela, b_ela = T[f"ela{s}"]
                        sl = slice(s * 128, (s + 1) * 128)
                        BT = bcT[:, gi * 2, sl]
                        CT = bcT[:, gi * 2 + 1, sl]
                        for hf, (pa, bpa) in enumerate(pas[s]):
                            S.op("dve", lambda e: e.tensor_copy(
                                out=ela[:, hf * 4:(hf + 1) * 4],
                                in_=pa[:, :].rearrange("p (a b) -> p a b", a=4)[:, :, 127]),
                                reads=[bpa], writes=[b_ela])
                            if not so:
                                for q in range(4):
                                    hh = hf * 4 + q
                                    S.op("act", lambda e: e.activation(out=dmat[:, hh, :], in_=pa[:, q * 128:(q + 1) * 128],
                                                                       func=AF.Relu, scale=-1.0,
                                                                       bias=acum[:, s, g * 8 + hh:g * 8 + hh + 1]),
                                         reads=[bpa, b_acum], writes=[b_dmat])
                        if not so:
                            S.op("act", lambda e: e.activation(out=dmat[:], in_=dmat[:], func=AF.Exp, scale=-1.0),
                                 reads=[b_dmat], writes=[b_dmat])
                        S.op("dve", lambda e: e.tensor_tensor(out=wst[:], in0=ela[:], in1=acum[:, s, hs], op=ALU.subtract),
                             reads=[b_ela, b_acum], writes=[b_wst])
                        S.op("act", lambda e: e.activation(out=wst[:], in_=wst[:], func=AF.Exp), reads=[b_wst],
                             writes=[b_wst])
                        S.op("act", lambda e: e.activation(out=ela[:], in_=ela[:], func=AF.Exp), reads=[b_ela],
                             writes=[b_ela])
                        S.op("dve", lambda e: e.tensor_tensor(
                            out=xw[:].rearrange("p (a b) -> p a b", a=8), in0=xdt[:].rearrange("p (a b) -> p a b", a=8),
                            in1=wst[:].unsqueeze(2).to_broadcast([128, 8, 64]), op=ALU.mult),
                            reads=[b_xdt, b_wst], writes=[b_xw])
                        if not so:
                            pcb, bpcb = psum()
                            S.op("pe", lambda e: e.matmul(pcb[:, 0:128], lhsT=BT, rhs=CT, start=True, stop=True),
                                 reads=[b_bcT], writes=[bpcb])
                            S.op("dve", lambda e: e.tensor_tensor(out=cbm[:], in0=pcb[:, 0:128], in1=tri[:], op=ALU.mult),
                                 reads=[bpcb, b_tri], writes=[b_cbm])
                            S.op("dve", lambda e: e.tensor_tensor(
                                out=scT[:], in0=dmat[:], in1=cbm[:].unsqueeze(1).to_broadcast([128, 8, 128]),
                                op=ALU.mult), reads=[b_dmat, b_cbm], writes=[b_scT])
                    for s in range(NS):
                        scT, b_scT = T[f"scT{s}"]
                        xdt, b_xdt = T[f"xdt{s}"]
                        xsd, b_xsd = T[f"xsd{s}"]
                        xw, b_xw = T[f"xw{s}"]
                        ela, b_ela = T[f"ela{s}"]
                        sl = slice(s * 128, (s + 1) * 128)
                        CT = bcT[:, gi * 2 + 1, sl]
                        if not so:
                            S.op("act", lambda e: e.copy(out=Stb[:], in_=St[:, g, :]), reads=[b_St], writes=[b_Stb])
                        S.op("dve", lambda e: e.tensor_tensor(
                            out=St[:, g, :].rearrange("p (a b) -> p a b", a=8),
                            in0=St[:, g, :].rearrange("p (a b) -> p a b", a=8),
                            in1=ela[:].unsqueeze(2).to_broadcast([128, 8, 64]), op=ALU.mult),
                            reads=[b_St, b_ela], writes=[b_St])
                        if not so:
                            py, bpy = psum()
                            for hh in range(8):
                                S.op("pe", lambda e: e.matmul(py[:, hh * 64:(hh + 1) * 64], lhsT=scT[:, hh, :],
                                                              rhs=xdt[:, hh * 64:(hh + 1) * 64], start=True, stop=False),
                                     reads=[b_scT, b_xdt], writes=[bpy], signal=False)
                                S.op("pe", lambda e: e.matmul(py[:, hh * 64:(hh + 1) * 64], lhsT=ident[:],
                                                              rhs=xsd[:, hh * 64:(hh + 1) * 64], start=False, stop=True),
                                     reads=[b_id, b_xsd], writes=[bpy], signal=(hh == 7))
                            pyi, bpyi = psum()
                            S.op("pe", lambda e: e.matmul(pyi[:, :], lhsT=CT, rhs=Stb[:], start=True, stop=True),
                                 reads=[b_bcT, b_Stb], writes=[bpyi])
                        pd, bpd = psum()
                        S.op("pe", lambda e: e.matmul(pd[:, :], lhsT=Btok[:, s, :], rhs=xw[:], start=True, stop=True),
                             reads=[b_Btok, b_xw], writes=[bpd])
                        if not so:
                            S.op("dve", lambda e: e.tensor_tensor(
                                out=ytmp[:].rearrange("p (a b) -> p a b", a=8), in0=pyi[:, :].rearrange("p (a b) -> p a b", a=8),
                                in1=expa[:, s, hs].unsqueeze(2).to_broadcast([128, 8, 64]), op=ALU.mult),
                                reads=[bpyi, b_expa], writes=[b_ytmp])
                            S.op("dve", lambda e: e.tensor_tensor(out=ytmp[:], in0=ytmp[:], in1=py[:, :], op=ALU.add),
                                 reads=[bpy, b_ytmp], writes=[b_ytmp])
                        S.op("dve", lambda e: e.tensor_tensor(out=St[:, g, :], in0=St[:, g, :], in1=pd[:, :], op=ALU.add),
                             reads=[b_St, bpd], writes=[b_St])
                        if not so:
                            S.op("dve", lambda e: e.tensor_tensor(out=tm_a[:, s, :], in0=ytmp[:], in1=tm_b[:, s, :],
                                                                   op=ALU.mult), reads=[b_ytmp, b_tmb], writes=[b_tma])
                            S.op("act", lambda e: e.activation(out=pjunk[:], in_=tm_a[:, s, :], func=AF.Square,
                                                               accum_out=ss[:, s:s + 1]), reads=[b_tma], writes=[b_pjunk, b_ss])

            def s_post(g):
                if True:
                    aT, baT = actT[g // 4]
                    if not so:
                        S.op("act", lambda e: e.activation(out=ss[:, 0:NS], in_=ss[:, 0:NS], func=AF.Ln, scale=1.0 / 512,
                                                           bias=EPS), reads=[b_ss], writes=[b_ss])
                        S.op("act", lambda e: e.activation(out=ss[:, 0:NS], in_=ss[:, 0:NS], func=AF.Exp, scale=-0.5),
                             reads=[b_ss], writes=[b_ss])
                        for s in range(NS):
                            S.op("dve", lambda e: e.tensor_scalar(out=onb[:, s, :], in0=tm_a[:, s, :], scalar1=ss[:, s:s + 1],
                                                                  scalar2=None, op0=ALU.mult),
                                 reads=[b_tma, b_ss], writes=[b_onb])
                            pass

            def s_post_b(g):
                aT, baT = actT[g // 4]
                if not so:
                    for s in range(NS):
                        transpose_to_actT(onb, b_onb, s, aT, baT, (g % 4) * 4, gnw[:, g * 4:(g + 1) * 4])
            emit_Rm(0)
            s_proj(0)
            s_tr(0)
            for g in range(8):
                s_scan(g)
                s_post(g)
                if g < 7:
                    s_proj(g + 1)
                s_post_b(g)
                if g < 7:
                    s_tr(g + 1)
            pes.close()
            if not so:
                for kb in range(2):
                    for nb in range(4):
                        outproj("sout", actT[kb][0], actT[kb][1], nb)

        if os_environ_get('KINFO'):
            print('SBUF bytes remaining/partition:', nc.sbuf_bytes_remaining)
        nwidx = {0: (0, 1), 1: (2, 3)}
        import os

        class _Stop(Exception):
            pass

        def ck(name):
            if os.environ.get("KSTOP") == name:
                raise _Stop()
        build_program.ck = ck
        try:
          for t in range(NT):
              is_main = t >= n_pre
              S.dma("sp", d_x, hres[:], x_d[t * TT:(t + 1) * TT, :].rearrange("(s p) d -> p s d", p=128),
                    writes=[b_hres])
              ck("load")
              for L in layers:
                  norm_to_uT(nwidx[L][0])
                  ck("norm")
                  pre_so = (not is_main) and L == 1
                  if L == 0:
                      gla()
                  else:
                      ssd(state_only=pre_so)
                  if not pre_so:
                      norm_to_uT(nwidx[L][1])
                      mlp()
              if t == n_pre - 1:
                  for (tt_, bb_) in ((Sg, b_Sg), (St, b_St), (cs, b_cs)):
                      flat = tt_[:].rearrange("p a b c -> p (a b c)") if tt_ is Sg else tt_[:].rearrange("p a b -> p (a b)")
                      S.op("dve", lambda e: e.tensor_scalar(out=flat, in0=flat, scalar1=flag[:, 0:1], scalar2=None,
                                                            op0=ALU.mult), reads=[bb_, b_const], writes=[bb_])
              if is_main:
                  tm = t - n_pre
                  for s in range(NS):
                      if final_norm:
                          S.op("act", lambda e: e.activation(out=ubf[:, 0, :], in_=hres[:, s, :], func=AF.Square,
                                                             accum_out=ss[:, s:s + 1]), reads=[b_hres], writes=b_ubfp[0] + [b_ss])
                          S.op("act", lambda e: e.activation(out=ss[:, s:s + 1], in_=ss[:, s:s + 1], func=AF.Ln, scale=1.0 / D,
                                                             bias=EPS), reads=[b_ss], writes=[b_ss])
                          S.op("act", lambda e: e.activation(out=ss[:, s:s + 1], in_=ss[:, s:s + 1], func=AF.Exp, scale=-0.5),
                               reads=[b_ss], writes=[b_ss])
                          S.op("dve", lambda e: e.scalar_tensor_tensor(out=hres[:, s, :], in0=hres[:, s, :], scalar=ss[:, s:s + 1],
                                                                       in1=fnw[:], op0=ALU.mult, op1=ALU.mult),
                               reads=[b_hres, b_ss, b_const], writes=[b_hres])
                          S.dma("sp", d_out, out_d[tm * TT + s * 128: tm * TT + (s + 1) * 128, :], hres[:, s, :],
                                reads=[b_hres])
                      else:
                          S.dma("sp", d_out, out_d[tm * TT + s * 128: tm * TT + (s + 1) * 128, :], hres[:, s, :],
                                reads=[b_hres])
        except _Stop:
            pass
        else:
            assert wstate["next"] == wstate["total"] == wstate["issued"]
        S.wait_all("sp", [d_out])
        S.wait_all("sp", ["pe", "dve", "act", "pool"] + d_w)
    return nc


def small_inputs(inputs):
    f = np.float32
    fm = lambda v: np.ascontiguousarray(v.reshape(-1, 128).T).astype(f)
    normw = np.concatenate([fm(inputs["mixer_norm_w"][0]), fm(inputs["mlp_norm_w"][0]),
                            fm(inputs["mixer_norm_w"][1]), fm(inputs["mlp_norm_w"][1])], axis=1)
    cwf = inputs["ssd_conv_w"][0]
    cbf = inputs["ssd_conv_b"][0]
    blocks = []
    for gp in range(4):
        g0, g1 = 2 * gp, 2 * gp + 1
        for base in (4096 + g0 * 128, 5120 + g0 * 128, 4096 + g1 * 128, 5120 + g1 * 128):
            blocks.append(np.arange(base, base + 128))
    for g in range(8):
        for fb in range(4):
            blocks.append(np.arange(g * 512 + fb * 128, g * 512 + fb * 128 + 128))
    convw = np.stack([cwf[:, b].T for b in blocks], axis=1)
    convb = np.stack([cbf[b] for b in blocks], axis=1)
    return {
        "normw": np.ascontiguousarray(normw, dtype=f),
        "fnw": inputs["final_norm_w"].reshape(1, D).astype(f),
        "onw": fm(inputs["gla_o_norm_w"][0]),
        "gnw": fm(inputs["ssd_gnorm_w"][0]),
        "bgk": inputs["gla_b_gk_up"][0].reshape(1, 1024).astype(f),
        "wgk": np.ascontiguousarray(inputs["gla_w_gk_up"][0], dtype=f),
        "headvec": np.concatenate([inputs["ssd_dt_bias"][0], inputs["ssd_a_log"][0],
                                   inputs["ssd_d_skip"][0]]).reshape(1, 192).astype(f),
        "convw": np.ascontiguousarray(convw.reshape(128, 48 * 4), dtype=f),
        "convb": np.ascontiguousarray(convb, dtype=f),
    }


def run_stage(inputs, xs_per_core, layers, n_pre, n_main, TT, final_norm, flags):
    wstream = build_wstream(inputs, layers)
    nc = build_program(layers, n_pre, n_main, TT, final_norm, int(wstream.size))
    small = small_inputs(inputs)
    in_maps = []
    for c, xc in enumerate(xs_per_core):
        m = dict(small)
        m["x"] = np.ascontiguousarray(xc, dtype=np.float32)
        m["wstream"] = wstream
        m["flag"] = np.full((128, 1), flags[c], np.float32)
        in_maps.append(m)
    res = run_bass_kernel_spmd(nc, in_maps, core_ids=list(range(len(xs_per_core))))
    return [r["out"] for r in res.results]


TT_DEFAULT = 256
FUSED = True


def kernel(**inputs):
    inputs = {k: np.asarray(v) for k, v in inputs.items()}
    x = inputs["x"]
    TT = TT_DEFAULT
    npre = HALF // TT
    nmain = HALF // TT
    flags = [float(c % 2) for c in range(NCORES)]

    def shard(h):
        xs = []
        for c in range(NCORES):
            b, half = c // 2, c % 2
            first = h[b, 0:HALF] if half == 1 else np.zeros((HALF, D), np.float32)
            mine = h[b, half * HALF:(half + 1) * HALF]
            xs.append(np.concatenate([first, mine], axis=0))
        return xs

    def gather(outs):
        o = np.empty((BATCH, SEQ, D), np.float32)
        for c in range(NCORES):
            b, half = c // 2, c % 2
            o[b, half * HALF:(half + 1) * HALF] = outs[c]
        return o

    if FUSED:
        return gather(run_stage(inputs, shard(x), [0, 1], npre, nmain, TT, True, flags))
    h1 = gather(run_stage(inputs, shard(x), [0], npre, nmain, TT, False, flags))
    return gather(run_stage(inputs, shard(h1), [1], npre, nmain, TT, True, flags))
```

```python
import numpy as np
import os
os_environ_get = os.environ.get
from contextlib import ExitStack
import concourse.bass as bass
import concourse.mybir as mybir
from concourse.bass_utils import run_bass_kernel_spmd

F32 = mybir.dt.float32
BF16 = mybir.dt.bfloat16
ALU = mybir.AluOpType
AF = mybir.ActivationFunctionType

D = 2048
NCORES = 8
SEQ = 4096
BATCH = 4
HALF = SEQ // 2
EPS = 1e-5


class Buf:
    __slots__ = ("name", "w", "r", "excl")

    def __init__(self, name, excl=False):
        self.name = name
        self.w = None
        self.r = {}
        self.excl = excl


class Sched:
    def __init__(self, nc, es, same_eng_sync=True):
        self.nc = nc
        self.es = es
        self.engs = {"pe": nc.tensor, "dve": nc.vector, "act": nc.scalar,
                     "pool": nc.gpsimd, "sp": nc.sync}
        self.sems = {}
        self.cnt = {}
        for k in self.engs:
            self.sems[k] = es.enter_context(nc.semaphore("sem_" + k))
            self.cnt[k] = 0
        self.waited = {k: {} for k in self.engs}
        self.same = same_eng_sync
        self.nops = 0

    def new_dma_sem(self, name):
        key = "dma_" + name
        self.sems[key] = self.es.enter_context(self.nc.semaphore(key))
        self.cnt[key] = 0
        return key

    def _deps(self, reads, writes, e=None):
        deps = {}
        for b in reads:
            if b.w is not None:
                k, v = b.w
                if deps.get(k, 0) < v:
                    deps[k] = v
            if b.excl:
                for k, v in b.r.items():
                    if k != e and deps.get(k, 0) < v:
                        deps[k] = v
        for b in writes:
            if b.w is not None:
                k, v = b.w
                if deps.get(k, 0) < v:
                    deps[k] = v
            for k, v in b.r.items():
                if deps.get(k, 0) < v:
                    deps[k] = v
        return deps

    def _wait(self, e, deps):
        for k, v in deps.items():
            if k == e and (not self.same or e == "pe"):
                continue
            if self.waited[e].get(k, 0) >= v:
                continue
            self.engs[e].wait_ge(self.sems[k], v)
            self.waited[e][k] = v

    def op(self, e, fn, reads=(), writes=(), signal=True):
        self._wait(e, self._deps(reads, writes, e))
        ins = fn(self.engs[e])
        self.nops += 1
        if signal:
            self.cnt[e] += 1
            ins.then_inc(self.sems[e], 1)
            v = self.cnt[e]
        else:
            v = self.cnt[e] + 1
        for b in reads:
            if b.r.get(e, 0) < v:
                b.r[e] = v
        for b in writes:
            b.w = (e, v)
            b.r = {}
        return ins

    def dma(self, q, dkey, out_ap, in_ap, reads=(), writes=(), **kw):
        self._wait(q, self._deps(reads, writes))
        ins = self.engs[q].dma_start(out=out_ap, in_=in_ap, **kw)
        self.cnt[dkey] += 16
        ins.then_inc(self.sems[dkey], 16)
        v = self.cnt[dkey]
        for b in reads:
            b.r[dkey] = v
        for b in writes:
            b.w = (dkey, v)
            b.r = {}
        return ins

    def wait_all(self, e, keys):
        for k in keys:
            v = self.cnt[k]
            if v > 0 and self.waited[e].get(k, 0) < v:
                self.engs[e].wait_ge(self.sems[k], v)
                self.waited[e][k] = v


def tile_specs(layers):
    sp = []
    ar = np.arange

    def mlp(i):
        for kb in range(4):
            for j in range(4):
                c0 = (4 * kb + j) * 512
                sp.append(("fc1", "mlp_w_fc1", i, 0, ar(c0, c0 + 512)))
            for nb in range(4):
                sp.append(("fc2", "mlp_w_fc2", i, kb * 2048, ar(nb * 512, nb * 512 + 512)))

    for L in layers:
        if L == 0:
            sp.append(("gr", "gla_w_in", 0, 0, ar(6144, 6160)))
            for h in range(4):
                sp.append(("qk", "gla_w_in", 0, 0,
                           np.concatenate([ar(h * 256, h * 256 + 256), ar(1024 + h * 256, 1024 + h * 256 + 256)])))
                sp.append(("v", "gla_w_in", 0, 0, ar(2048 + h * 512, 2048 + h * 512 + 512)))
                sp.append(("g", "gla_w_in", 0, 0, ar(4096 + h * 512, 4096 + h * 512 + 512)))
            for nb in range(4):
                sp.append(("gout", "gla_w_out", 0, 0, ar(nb * 512, nb * 512 + 512)))
            mlp(0)
        else:
            sp.append(("dt", "ssd_w_in", 0, 0, ar(10240, 10304)))
            for gp in range(4):
                g0, g1 = 2 * gp, 2 * gp + 1
                sp.append(("bc", "ssd_w_in", 0, 0, np.concatenate([
                    ar(8192 + g0 * 128, 8192 + g0 * 128 + 128), ar(9216 + g0 * 128, 9216 + g0 * 128 + 128),
                    ar(8192 + g1 * 128, 8192 + g1 * 128 + 128), ar(9216 + g1 * 128, 9216 + g1 * 128 + 128)])))
                for g in (g0, g1):
                    sp.append(("xs", "ssd_w_in", 0, 0, ar(4096 + g * 512, 4096 + g * 512 + 512)))
                    sp.append(("z", "ssd_w_in", 0, 0, ar(g * 512, g * 512 + 512)))
            for kb in range(2):
                for nb in range(4):
                    sp.append(("sout", "ssd_w_out", 0, kb * 2048, ar(nb * 512, nb * 512 + 512)))
            mlp(1)
    return sp


def build_wstream(inputs, layers):
    parts = []
    for (_, key, slot, r0, cols) in tile_specs(layers):
        w = inputs[key][slot]
        blk = w[r0:r0 + 2048][:, cols]
        parts.append(np.ascontiguousarray(
            blk.reshape(16, 128, len(cols)).transpose(1, 0, 2)).reshape(-1))
    return np.concatenate(parts).astype(np.float32, copy=False)


def build_program(layers, n_pre, n_main, TT, final_norm, wtotal, nwb=3):
    NS = TT // 128
    NT = n_pre + n_main
    nc = bass.Bass("TRN2", target_bir_lowering=False)
    specs = tile_specs(layers)
    x_d = nc.dram_tensor("x", [NT * TT, D], F32, kind="ExternalInput").ap()
    w_d = nc.dram_tensor("wstream", [wtotal], F32, kind="ExternalInput").ap()
    normw_d = nc.dram_tensor("normw", [128, 4 * 16], F32, kind="ExternalInput").ap()
    fnw_d = nc.dram_tensor("fnw", [1, D], F32, kind="ExternalInput").ap()
    onw_d = nc.dram_tensor("onw", [128, 4], F32, kind="ExternalInput").ap()
    gnw_d = nc.dram_tensor("gnw", [128, 32], F32, kind="ExternalInput").ap()
    bgk_d = nc.dram_tensor("bgk", [1, 1024], F32, kind="ExternalInput").ap()
    wgk_d = nc.dram_tensor("wgk", [16, 1024], F32, kind="ExternalInput").ap()
    hv_d = nc.dram_tensor("headvec", [1, 3 * 64], F32, kind="ExternalInput").ap()
    cw_d = nc.dram_tensor("convw", [128, 48 * 4], F32, kind="ExternalInput").ap()
    cb_d = nc.dram_tensor("convb", [128, 48], F32, kind="ExternalInput").ap()
    flag_d = nc.dram_tensor("flag", [128, 1], F32, kind="ExternalInput").ap()
    out_d = nc.dram_tensor("out", [n_main * TT, D], F32, kind="ExternalOutput").ap()
    wbf_d = nc.dram_tensor("wbf16", [wtotal], BF16, kind="Internal").ap()

    es = ExitStack()
    with es:
        es.enter_context(nc.allow_low_precision("bf16 matmul operands, fp32 accumulation"))
        S = Sched(nc, es)

        def sb(name, shape, dt):
            return es.enter_context(nc.sbuf_tensor("sb_" + name, shape, dt)), Buf(name)

        hres, b_hres = sb("hres", [128, NS, D], F32)
        uT, b_uT = sb("uT", [128, 16, TT], BF16)
        wbufs = [sb(f"wb{i}", [128, 16, 512], BF16) for i in range(nwb)]
        actT = [sb(f"actT{i}", [128, 16, TT], BF16) for i in range(2)]
        tm_a, b_tma = sb("tm_a", [128, NS, 512], F32)
        tm_b, b_tmb = sb("tm_b", [128, NS, 512], F32)
        ubf, b_ubf = sb("ubf", [128, NS, D], BF16)
        b_ubfp = [[Buf(f"ubfp{j}_{i}") for i in range(4)] for j in range(NS)]
        nss, _ = sb("nss", [128, NS], F32)
        b_nss = [Buf(f"nss{i}") for i in range(NS)]
        ss, b_ss = sb("ss", [128, 8], F32)
        ident, b_id = sb("ident", [128, 128], BF16)
        tri, b_tri = sb("tri", [128, 128], F32)
        triU, b_triU = sb("triU", [128, 128], F32)
        ones, b_ones = sb("ones", [128, 128], F32)
        normw, b_normw = sb("normw", [128, 4, 16], F32)
        fnw, b_fnw = sb("fnw", [128, D], F32)
        onw, b_onw = sb("onw", [128, 4], F32)
        gnw, b_gnw = sb("gnw", [128, 32], F32)
        bgk, b_bgk = sb("bgk", [128, 1024], F32)
        wgk, b_wgk = sb("wgk", [16, 1024], BF16)
        hv, b_hv = sb("hv", [128, 3, 64], F32)
        negA, b_negA = sb("negA", [128, 64], F32)
        cw, b_cw = sb("cw", [128, 48, 4], F32)
        cb, b_cb = sb("cb", [128, 48], F32)
        flag, b_flag = sb("flag", [128, 1], F32)
        Sg, b_Sg = sb("Sg", [128, 4, 2, 512], F32)
        St, b_St = sb("St", [128, 8, 512], F32)
        cs, b_cs = sb("cs", [128, 48, 3], F32)
        onb, b_onb = sb("onb", [128, NS, 512], BF16)
        rtmp, b_rtmp = sb("rtmp", [128, TT], F32)
        phase_ctr = [0]
        prev_phase_bufs = []

        def phase_alloc(pes, decls):
            phase_ctr[0] += 1
            out = {}
            for name, shape, dt in decls:
                t = pes.enter_context(nc.sbuf_tensor(f"ph{phase_ctr[0]}_{name}", shape, dt))
                out[name] = (t, Buf(name))
            for e in ("pe", "dve", "act", "pool"):
                S._wait(e, S._deps([], prev_phase_bufs, e))
            prev_phase_bufs[:] = [bb for (_, bb) in out.values()]
            return out

        NPS = 6
        psb = []
        for i in range(NPS):
            t = es.enter_context(nc.psum_tensor(f"ps{i}", [128, 512], F32))
            psb.append((t, Buf(f"ps{i}", excl=True)))
        pT = []
        for i in range(2):
            t = es.enter_context(nc.psum_tensor(f"psT{i}", [128, 8, 128], BF16))
            pT.append((t, 0, Buf(f"psT{i}", excl=True)))
        ps_i = [0]
        pt_i = [0]

        def psum():
            t, b = psb[ps_i[0] % NPS]
            ps_i[0] += 1
            return t, b

        def psumT():
            r = pT[pt_i[0] % 2]
            pt_i[0] += 1
            return r

        d_const = S.new_dma_sem("const")
        d_x = S.new_dma_sem("x")
        d_out = S.new_dma_sem("out")
        d_w = [S.new_dma_sem(f"w{i}") for i in range(nwb)]

        b_const = Buf("consts")
        cl = [(normw[:].rearrange("p a b -> p (a b)"), normw_d), (fnw[:], fnw_d.to_broadcast([128, D])),
              (onw[:], onw_d), (gnw[:], gnw_d), (bgk[:], bgk_d.to_broadcast([128, 1024])),
              (hv[:].rearrange("p a b -> p (a b)"), hv_d.to_broadcast([128, 192])),
              (cw[:].rearrange("p a b -> p (a b)"), cw_d), (cb[:], cb_d), (flag[:], flag_d)]
        for o_ap, i_ap in cl:
            S.dma("sp", d_const, o_ap, i_ap, writes=[b_const])
        b_wgk2 = Buf("wgk2")
        S.dma("pool", S.new_dma_sem("wgk"), wgk[:], wgk_d, writes=[b_wgk2])
        S.op("pool", lambda e: e.memset(ident[:], 1.0), writes=[b_id])
        S.op("pool", lambda e: e.affine_select(out=ident[:], in_=ident[:], pattern=[[-1, 128]], compare_op=ALU.is_equal,
                                               fill=0.0, base=0, channel_multiplier=1), reads=[b_id], writes=[b_id])
        S.op("pool", lambda e: e.memset(tri[:], 1.0), writes=[b_tri])
        S.op("pool", lambda e: e.affine_select(out=tri[:], in_=tri[:], pattern=[[1, 128]], compare_op=ALU.is_ge,
                                               fill=0.0, base=0, channel_multiplier=-1), reads=[b_tri], writes=[b_tri])
        S.op("pool", lambda e: e.memset(triU[:], 1.0), writes=[b_triU])
        S.op("pool", lambda e: e.affine_select(out=triU[:], in_=triU[:], pattern=[[-1, 128]], compare_op=ALU.is_gt,
                                               fill=0.0, base=0, channel_multiplier=1), reads=[b_triU], writes=[b_triU])
        S.op("pool", lambda e: e.memset(ones[:], 1.0), writes=[b_ones])
        S.op("pool", lambda e: e.memset(Sg[:], 0.0), writes=[b_Sg])
        S.op("pool", lambda e: e.memset(St[:], 0.0), writes=[b_St])
        S.op("pool", lambda e: e.memset(cs[:], 0.0), writes=[b_cs])
        S.op("act", lambda e: e.activation(out=negA[:], in_=hv[:, 1, :], func=AF.Exp), reads=[b_const], writes=[b_negA])
        S.op("dve", lambda e: e.tensor_scalar(out=negA[:], in0=negA[:], scalar1=-1.0, scalar2=None, op0=ALU.mult),
             reads=[b_negA], writes=[b_negA])
        CONST = [b_const, b_id, b_tri, b_triU, b_ones, b_negA]

        offs = []
        o = 0
        for sp_ in specs:
            offs.append(o)
            o += 2048 * len(sp_[4])
        assert o == wtotal, (o, wtotal)
        def skip_in_prefix(sp_):
            return (sp_[0] in ("z", "sout")) or (sp_[0] in ("fc1", "fc2") and sp_[2] == 1)
        seq = []
        for t in range(NT):
            for i, sp_ in enumerate(specs):
                if t < n_pre and skip_in_prefix(sp_):
                    continue
                seq.append(i)
        wstate = {"issued": 0, "next": 0, "total": len(seq)}
        d_ws = [S.new_dma_sem(f"ws{i}") for i in range(nwb)]
        b_scr = [Buf(f"scr{i}") for i in range(len(specs))]
        converted = [False] * len(specs)

        def issue_w(n):
            t = seq[n]
            ncols = len(specs[t][4])
            wt, bw = wbufs[n % nwb]
            if not converted[t]:
                src = w_d[offs[t]: offs[t] + 2048 * ncols].rearrange("(p k n) -> p k n", p=128, k=16)
                S.dma("pool", d_w[n % nwb], wt[:, :, 0:ncols], src, writes=[bw])
                if NT > 1:
                    dst = wbf_d[offs[t]: offs[t] + 2048 * ncols].rearrange("(p k n) -> p k n", p=128, k=16)
                    S.dma("sp", d_ws[n % nwb], dst, wt[:, :, 0:ncols], reads=[bw], writes=[b_scr[t]])
                converted[t] = True
            else:
                src = wbf_d[offs[t]: offs[t] + 2048 * ncols].rearrange("(p k n) -> p k n", p=128, k=16)
                S.dma("sp", d_wl[n % nwb], wt[:, :, 0:ncols], src, reads=[b_scr[t]], writes=[bw])

        d_wl = [S.new_dma_sem(f"wl{i}") for i in range(nwb)]

        def next_tile(kind):
            n = wstate["next"]
            assert specs[seq[n]][0] == kind, (specs[seq[n]][0], kind)
            while wstate["issued"] < min(n + nwb, wstate["total"]):
                issue_w(wstate["issued"])
                wstate["issued"] += 1
            wstate["next"] += 1
            return wbufs[n % nwb]

        def rstd_from_ss(n):
            pass

        def norm_to_uT(widx):
            for s in range(NS):
                S.op("act", lambda e: e.activation(out=ubf[:, s, :], in_=hres[:, s, :], func=AF.Square,
                                                   accum_out=nss[:, s:s + 1]),
                     reads=[b_hres], writes=b_ubfp[s] + [b_nss[s]])
                S.op("act", lambda e: e.activation(out=nss[:, s:s + 1], in_=nss[:, s:s + 1], func=AF.Ln, scale=1.0 / D,
                                                   bias=EPS), reads=[b_nss[s]], writes=[b_nss[s]])
                S.op("act", lambda e: e.activation(out=nss[:, s:s + 1], in_=nss[:, s:s + 1], func=AF.Exp, scale=-0.5),
                     reads=[b_nss[s]], writes=[b_nss[s]])
            def piece(k):
                s, grp = divmod(k, 4)
                S.op("dve",
                     lambda e: e.tensor_scalar(out=ubf[:, s, grp * 512:(grp + 1) * 512],
                                               in0=hres[:, s, grp * 512:(grp + 1) * 512],
                                               scalar1=nss[:, s:s + 1], scalar2=None, op0=ALU.mult),
                     reads=[b_hres, b_nss[s]], writes=[b_ubfp[s][grp]])
            NG = NS * 4
            piece(0)
            piece(1)
            for k in range(NG):
                s, grp = divmod(k, 4)
                pt, po, bpt = psumT()
                for j in range(4):
                    kc = grp * 4 + j
                    S.op("pe", lambda e: e.transpose(out=pt[:, po + j, :], in_=ubf[:, s, kc * 128:(kc + 1) * 128],
                                                     identity=ident[:]),
                         reads=[b_ubfp[s][grp], b_id], writes=[bpt], signal=(j == 3))
                if k + 2 < NG:
                    piece(k + 2)
                if k % 2 == 0:
                    S.op("dve", lambda e: e.tensor_tensor(
                        out=uT[:, grp * 4:(grp + 1) * 4, s * 128:(s + 1) * 128], in0=pt[:, po:po + 4, :],
                        in1=normw[:, widx, grp * 4:(grp + 1) * 4].unsqueeze(2).to_broadcast([128, 4, 128]),
                        op=ALU.mult), reads=[bpt, b_const], writes=[b_uT])
                else:
                    for j in range(4):
                        kc = grp * 4 + j
                        S.op("act", lambda e: e.mul(out=uT[:, kc, s * 128:(s + 1) * 128], in_=pt[:, po + j, :],
                                                    mul=normw[:, widx, kc:kc + 1]),
                             reads=[bpt, b_const], writes=[b_uT])

        def proj_fm(wt, bw, c0, ncol, evac):
            ps, bps = psum()
            for kc in range(16):
                S.op("pe", lambda e: e.matmul(ps[0:ncol, 0:TT], lhsT=wt[:, kc, c0:c0 + ncol], rhs=uT[:, kc, :],
                                              start=(kc == 0), stop=(kc == 15)),
                     reads=[bw, b_uT], writes=[bps], signal=(kc == 15))
            evac(ps, bps)

        def proj_tm(wt, bw, src, bsrc, s, c0, ncol, evac):
            ps, bps = psum()
            for kc in range(16):
                S.op("pe", lambda e: e.matmul(ps[:, 0:ncol], lhsT=src[:, kc, s * 128:(s + 1) * 128],
                                              rhs=wt[:, kc, c0:c0 + ncol], start=(kc == 0), stop=(kc == 15)),
                     reads=[bw, bsrc], writes=[bps], signal=(kc == 15))
            evac(ps, bps)

        def outproj(kind, src, bsrc, nb):
            wt, bw = next_tile(kind)
            for s in range(NS):
                def ev(ps, bps, s=s):
                    S.op("dve", lambda e: e.tensor_tensor(out=hres[:, s, nb * 512:(nb + 1) * 512],
                                                          in0=hres[:, s, nb * 512:(nb + 1) * 512], in1=ps[:, :],
                                                          op=ALU.add),
                         reads=[bps, b_hres], writes=[b_hres])
                proj_tm(wt, bw, src, bsrc, s, 0, 512, ev)

        def mlp():
            for kb in range(4):
                at, bat = actT[kb % 2]
                for j in range(4):
                    wt, bw = next_tile("fc1")
                    for fb in range(4):
                        def ev(ps, bps, j=j, fb=fb):
                            S.op("act", lambda e: e.activation(out=rtmp[:], in_=ps[:, 0:TT], func=AF.Relu),
                                 reads=[bps], writes=[b_rtmp])
                            S.op("dve", lambda e: e.tensor_tensor(out=at[:, j * 4 + fb, :], in0=rtmp[:], in1=rtmp[:],
                                                                  op=ALU.mult),
                                 reads=[b_rtmp], writes=[bat])
                        proj_fm(wt, bw, fb * 128, 128, ev)
                for nb in range(4):
                    outproj("fc2", at, bat, nb)

        def transpose_to_actT(srcb, bsrc, s, dst, bdst, kc0, scale_ap):
            pt, po, bpt = psumT()
            for ec in range(4):
                S.op("pe", lambda e: e.transpose(out=pt[:, po + ec, :], in_=srcb[:, s, ec * 128:(ec + 1) * 128],
                                                 identity=ident[:]),
                     reads=[bsrc, b_id], writes=[bpt], signal=(ec == 3))
            S.op("dve", lambda e: e.tensor_tensor(
                out=dst[:, kc0:kc0 + 4, s * 128:(s + 1) * 128], in0=pt[:, po:po + 4, :],
                in1=scale_ap.unsqueeze(2).to_broadcast([128, 4, 128]), op=ALU.mult),
                reads=[bpt, b_const], writes=[bdst])

        def gla():
            aT, baT = actT[0]
            pes = ExitStack()
            dec = [("lsp0", [128, 1024], F32), ("lsp1", [128, 1024], F32), ("grT", [16, TT], BF16),
                   ("qT", [128, 2, TT], F32), ("kT", [128, 2, TT], F32), ("ktok", [128, NS, 256], F32),
                   ("vtok", [128, NS, 512], BF16), ("Sb", [128, 2, 512], BF16), ("tpre", [128, 512], F32),
                   ("junk", [128, 512], BF16)]
            for i in range(NS):
                dec += [(f"Eq{i}", [128, 2, 128], F32), (f"Ek{i}", [128, 2, 128], F32), (f"Ed{i}", [128, 256], F32),
                        (f"qtl{i}", [128, 2, 128], BF16), (f"ktl{i}", [128, 2, 128], BF16),
                        (f"kdec{i}", [128, 256], BF16), (f"attT{i}", [128, 128], BF16)]
            T = phase_alloc(pes, dec)
            lsps = [T["lsp0"], T["lsp1"]]
            grT, b_grT = T["grT"]
            qT, b_qT = T["qT"]
            kT, b_kT = T["kT"]
            ktok, b_ktok = T["ktok"]
            vtok, b_vtok = T["vtok"]
            Sb, b_Sb = T["Sb"]
            tpre, b_tpre = T["tpre"]
            pjunk, b_pjunk = T["junk"]
            wt, bw = next_tile("gr")

            def ev_gr(ps, bps):
                S.op("act", lambda e: e.copy(out=grT[:], in_=ps[0:16, 0:TT]), reads=[bps], writes=[b_grT])
            proj_fm(wt, bw, 0, 16, ev_gr)
            ck("gr")
            for s in range(NS):
                for hf in range(2):
                    ps, bps = psum()
                    S.op("pe", lambda e: e.matmul(ps[:, :], lhsT=grT[0:16, s * 128:(s + 1) * 128],
                                                  rhs=wgk[0:16, hf * 512:(hf + 1) * 512], start=True, stop=True),
                         reads=[b_grT, b_wgk2], writes=[bps])
                    S.op("dve", lambda e: e.tensor_tensor(out=tpre[:], in0=ps[:, :], in1=bgk[:, hf * 512:(hf + 1) * 512],
                                                          op=ALU.add), reads=[bps, b_const], writes=[b_tpre])
                    S.op("act", lambda e: e.activation(out=tpre[:], in_=tpre[:], func=AF.Exp, scale=-1.0),
                         reads=[b_tpre], writes=[b_tpre])
                    S.op("act", lambda e: e.activation(out=lsps[s][0][:, hf * 512:(hf + 1) * 512], in_=tpre[:], func=AF.Ln,
                                                       bias=1.0), reads=[b_tpre], writes=[lsps[s][1]])
            ck("lsp")
            def g_qkv(h):
                wt, bw = next_tile("qk")
                for fb in range(4):
                    dstT, bd = (qT, b_qT) if fb < 2 else (kT, b_kT)

                    def ev(ps, bps, fb=fb, dstT=dstT, bd=bd):
                        S.op("act", lambda e: e.copy(out=dstT[:, fb % 2, :], in_=ps[:, 0:TT]), reads=[bps], writes=[bd])
                    proj_fm(wt, bw, fb * 128, 128, ev)
                for s in range(NS):
                    def ev(ps, bps, s=s):
                        S.op("act", lambda e: e.copy(out=ktok[:, s, :], in_=ps[:, 0:256]), reads=[bps], writes=[b_ktok])
                    proj_tm(wt, bw, uT, b_uT, s, 256, 256, ev)
                ck("qk")
                wt, bw = next_tile("v")
                for s in range(NS):
                    def ev(ps, bps, s=s):
                        S.op("act", lambda e: e.copy(out=vtok[:, s, :], in_=ps[:, :]), reads=[bps], writes=[b_vtok])
                    proj_tm(wt, bw, uT, b_uT, s, 0, 512, ev)
            def g_g(h):
                wt, bw = next_tile("g")
                for s in range(NS):
                    def ev(ps, bps, s=s):
                        S.op("act", lambda e: e.activation(out=tm_b[:, s, :], in_=ps[:, :], func=AF.Silu),
                             reads=[bps], writes=[b_tmb])
                    proj_tm(wt, bw, uT, b_uT, s, 0, 512, ev)
            def g_scan(h):
                for s in range(NS):
                    Eq, b_Eq = T[f"Eq{s}"]
                    Ek, b_Ek = T[f"Ek{s}"]
                    Ed, b_Ed = T[f"Ed{s}"]
                    qtl, b_qtl = T[f"qtl{s}"]
                    ktl, b_ktl = T[f"ktl{s}"]
                    kdec, b_kdec = T[f"kdec{s}"]
                    attT, b_attT = T[f"attT{s}"]
                    sl = slice(s * 128, (s + 1) * 128)
                    lh = lsps[s][0][:, h * 256:(h + 1) * 256]
                    b_lsp_s = lsps[s][1]
                    pc, bpc = psum()
                    for dc in range(2):
                        S.op("pe", lambda e: e.matmul(pc[:, dc * 128:(dc + 1) * 128], lhsT=lh[:, dc * 128:(dc + 1) * 128],
                                                      rhs=tri[:], start=True, stop=True),
                             reads=[b_lsp_s, b_tri], writes=[bpc], signal=False)
                    S.op("pe", lambda e: e.matmul(pc[:, 256:512], lhsT=triU[:], rhs=lh, start=True, stop=True),
                         reads=[b_lsp_s, b_triU], writes=[bpc])
                    S.op("act", lambda e: e.activation(out=Eq[:].rearrange("p a b -> p (a b)"), in_=pc[:, 0:256],
                                                       func=AF.Exp, scale=-1.0 / 16), reads=[bpc], writes=[b_Eq])
                    S.op("act", lambda e: e.activation(out=Ek[:].rearrange("p a b -> p (a b)"), in_=pc[:, 0:256],
                                                       func=AF.Exp, scale=1.0 / 16), reads=[bpc], writes=[b_Ek])
                    S.op("act", lambda e: e.activation(out=Ed[:], in_=pc[:, 256:512], func=AF.Exp, scale=-1.0 / 16),
                         reads=[bpc], writes=[b_Ed])
                    ck("s1")
                    S.op("dve", lambda e: e.scalar_tensor_tensor(out=qtl[:], in0=qT[:, :, sl], scalar=1.0 / 16, in1=Eq[:],
                                                                 op0=ALU.mult, op1=ALU.mult),
                         reads=[b_qT, b_Eq], writes=[b_qtl])
                    S.op("dve", lambda e: e.tensor_tensor(out=ktl[:], in0=kT[:, :, sl], in1=Ek[:], op=ALU.mult),
                         reads=[b_kT, b_Ek], writes=[b_ktl])
                    S.op("dve", lambda e: e.tensor_tensor(out=kdec[:], in0=ktok[:, s, :], in1=Ed[:], op=ALU.mult),
                         reads=[b_ktok, b_Ed], writes=[b_kdec])
                    ck("s2")
                    pa, bpa = psum()
                    for dc in range(2):
                        S.op("pe", lambda e: e.matmul(pa[:, 0:128], lhsT=ktl[:, dc, :], rhs=qtl[:, dc, :],
                                                      start=(dc == 0), stop=(dc == 1)),
                             reads=[b_ktl, b_qtl], writes=[bpa], signal=(dc == 1))
                    S.op("dve", lambda e: e.tensor_tensor(out=attT[:], in0=pa[:, 0:128], in1=tri[:], op=ALU.mult),
                         reads=[bpa, b_tri], writes=[b_attT])
                for s in range(NS):
                    Eq, b_Eq = T[f"Eq{s}"]
                    Ek, b_Ek = T[f"Ek{s}"]
                    Ed, b_Ed = T[f"Ed{s}"]
                    qtl, b_qtl = T[f"qtl{s}"]
                    ktl, b_ktl = T[f"ktl{s}"]
                    kdec, b_kdec = T[f"kdec{s}"]
                    attT, b_attT = T[f"attT{s}"]
                    sl = slice(s * 128, (s + 1) * 128)
                    S.op("act", lambda e: e.copy(out=Sb[:], in_=Sg[:, h, :, :]), reads=[b_Sg], writes=[b_Sb])
                    ck("s3")
                    po_, bpo = psum()
                    S.op("pe", lambda e: e.matmul(po_[:, :], lhsT=attT[:], rhs=vtok[:, s, :], start=True, stop=False),
                         reads=[b_attT, b_vtok], writes=[bpo], signal=False)
                    for dc in range(2):
                        S.op("pe", lambda e: e.matmul(po_[:, :], lhsT=qtl[:, dc, :], rhs=Sb[:, dc, :], start=False,
                                                      stop=(dc == 1)),
                             reads=[b_qtl, b_Sb], writes=[bpo], signal=(dc == 1))
                    S.op("dve", lambda e: e.tensor_copy(out=tm_a[:, s, :], in_=po_[:, :]), reads=[bpo], writes=[b_tma])
                    S.op("act", lambda e: e.activation(out=pjunk[:], in_=po_[:, :], func=AF.Square,
                                                       accum_out=ss[:, s:s + 1]), reads=[bpo], writes=[b_pjunk, b_ss])
                    ck("s4")
                    for dc in range(2):
                        pd, bpd = psum()
                        S.op("pe", lambda e: e.matmul(pd[:, :], lhsT=kdec[:, dc * 128:(dc + 1) * 128], rhs=vtok[:, s, :],
                                                      start=True, stop=True), reads=[b_kdec, b_vtok], writes=[bpd])
                        S.op("dve", lambda e: e.scalar_tensor_tensor(out=Sg[:, h, dc, :], in0=Sg[:, h, dc, :],
                                                                     scalar=Eq[:, dc, 127:128], in1=pd[:, :],
                                                                     op0=ALU.mult, op1=ALU.add),
                             reads=[b_Sg, b_Eq, bpd], writes=[b_Sg])
            def g_post(h):
                S.op("act", lambda e: e.activation(out=ss[:, 0:NS], in_=ss[:, 0:NS], func=AF.Ln, scale=1.0 / 512, bias=EPS),
                     reads=[b_ss], writes=[b_ss])
                S.op("act", lambda e: e.activation(out=ss[:, 0:NS], in_=ss[:, 0:NS], func=AF.Exp, scale=-0.5),
                     reads=[b_ss], writes=[b_ss])
                for s in range(NS):
                    S.op("dve", lambda e: e.scalar_tensor_tensor(out=onb[:, s, :], in0=tm_a[:, s, :], scalar=ss[:, s:s + 1],
                                                                 in1=tm_b[:, s, :], op0=ALU.mult, op1=ALU.mult),
                         reads=[b_tma, b_tmb, b_ss], writes=[b_onb])

            def g_post_b(h):
                for s in range(NS):
                    transpose_to_actT(onb, b_onb, s, aT, baT, h * 4, onw[:, 0:4])
            g_qkv(0)
            g_g(0)
            for h in range(4):
                g_scan(h)
                g_post(h)
                if h < 3:
                    g_qkv(h + 1)
                    g_g(h + 1)
                g_post_b(h)
            ck("heads")
            pes.close()
            for nb in range(4):
                outproj("gout", aT, baT, nb)
            ck("gout")

        def ssd(state_only=False):
            so = state_only
            pes = ExitStack()
            dec = [("dtv", [128, NS, 64], F32), ("dtA", [128, NS, 64], F32), ("acum", [128, NS, 64], F32),
                   ("expa", [128, NS, 64], F32), ("nacum", [128, NS, 64], F32), ("pre", [128, 4, TT + 3], F32), ("cacc", [128, 4, TT], F32),
                   ("bcT", [128, 4, TT], BF16), ("xsT", [128, 4, TT], BF16), ("xstok", [128, NS, 512], BF16),
                   ("Btok", [128, NS, 128], BF16), ("Stb", [128, 512], BF16), ("ytmp", [128, 512], F32),
                   ("junk", [128, 512], BF16)]
            for i in range(NS):
                dec += [(f"Rm{i}", [128, 1024], F32), (f"dmat{i}", [128, 8, 128], F32), (f"scT{i}", [128, 8, 128], BF16),
                        (f"cbm{i}", [128, 128], F32), (f"xdt{i}", [128, 512], BF16), (f"xw{i}", [128, 512], BF16),
                        (f"wst{i}", [128, 8], F32), (f"ela{i}", [128, 8], F32), (f"xsd{i}", [128, 512], BF16)]
            T = phase_alloc(pes, dec)
            dtv, b_dtv = T["dtv"]
            dtA, b_dtA = T["dtA"]
            acum, b_acum = T["acum"]
            expa, b_expa = T["expa"]
            nacum, b_nacum = T["nacum"]
            pre, b_pre = T["pre"]
            cacc, b_cacc = T["cacc"]
            b_pres = [Buf(f"pre{i}") for i in range(4)]
            b_caccs = [Buf(f"cacc{i}") for i in range(4)]
            prev_phase_bufs.extend(b_pres + b_caccs)
            bcT, b_bcT = T["bcT"]
            xsT, b_xsT = T["xsT"]
            xstok, b_xstok = T["xstok"]
            Btok, b_Btok = T["Btok"]
            Stb, b_Stb = T["Stb"]
            ytmp, b_ytmp = T["ytmp"]
            pjunk, b_pjunk = T["junk"]

            def conv_tile(kind, blk0, dst, bdst):
                wt, bw = next_tile(kind)
                pend = [None]

                def conv_s(blk, slot):
                    S.op("act", lambda e: e.activation(out=dst[:, slot, :], in_=cacc[:, slot, :], func=AF.Silu,
                                                       bias=cb[:, blk:blk + 1]),
                         reads=[b_caccs[slot], b_const], writes=[bdst])

                for fb in range(4):
                    def ev(ps, bps, fb=fb):
                        blk, slot = blk0 + fb, fb
                        S.op("act", lambda e: e.copy(out=pre[:, slot, 0:3], in_=cs[:, blk, :]), reads=[b_cs],
                             writes=[b_pres[slot]])
                        S.op("act", lambda e: e.copy(out=pre[:, slot, 3:3 + TT], in_=ps[:, 0:TT]), reads=[bps],
                             writes=[b_pres[slot]])
                        S.op("act", lambda e: e.mul(out=cacc[:, slot, :], in_=ps[:, 0:TT], mul=cw[:, blk, 3:4]),
                             reads=[bps, b_const], writes=[b_caccs[slot]])
                        S.op("act", lambda e: e.copy(out=cs[:, blk, :], in_=pre[:, slot, TT:TT + 3]),
                             reads=[b_pres[slot]], writes=[b_cs])
                        if pend[0] is not None:
                            conv_s(*pend[0])
                        for k in range(3):
                            S.op("dve", lambda e: e.scalar_tensor_tensor(out=cacc[:, slot, :], in0=pre[:, slot, k:k + TT],
                                                                         scalar=cw[:, blk, k:k + 1], in1=cacc[:, slot, :],
                                                                         op0=ALU.mult, op1=ALU.add),
                                 reads=[b_pres[slot], b_caccs[slot], b_const], writes=[b_caccs[slot]])
                        pend[0] = (blk, slot)
                    proj_fm(wt, bw, fb * 128, 128, ev)
                conv_s(*pend[0])

            wt, bw = next_tile("dt")
            for s in range(NS):
                def ev(ps, bps, s=s):
                    S.op("dve", lambda e: e.tensor_tensor(out=dtv[:, s, :], in0=ps[:, 0:64], in1=hv[:, 0, :], op=ALU.add),
                         reads=[bps, b_const], writes=[b_dtv])
                proj_tm(wt, bw, uT, b_uT, s, 0, 64, ev)
            S.op("act", lambda e: e.activation(out=dtv[:], in_=dtv[:], func=AF.Exp), reads=[b_dtv], writes=[b_dtv])
            S.op("act", lambda e: e.activation(out=dtv[:], in_=dtv[:], func=AF.Ln, bias=1.0), reads=[b_dtv], writes=[b_dtv])
            for s in range(NS):
                S.op("dve", lambda e: e.tensor_tensor(out=dtA[:, s, :], in0=dtv[:, s, :], in1=negA[:], op=ALU.mult),
                     reads=[b_dtv, b_negA], writes=[b_dtA])
                ps, bps = psum()
                S.op("pe", lambda e: e.matmul(ps[:, 0:64], lhsT=tri[:], rhs=dtA[:, s, :], start=True, stop=True),
                     reads=[b_tri, b_dtA], writes=[bps])
                S.op("dve", lambda e: e.tensor_copy(out=acum[:, s, :], in_=ps[:, 0:64]), reads=[bps], writes=[b_acum])
                S.op("act", lambda e: e.activation(out=expa[:, s, :], in_=ps[:, 0:64], func=AF.Exp), reads=[bps],
                     writes=[b_expa])
            def s_proj(g):
                gp, gi = g // 2, g % 2
                if gi == 0:
                    conv_tile("bc", gp * 4, bcT, b_bcT)
                conv_tile("xs", 16 + g * 4, xsT, b_xsT)
                if not so:
                    wt, bw = next_tile("z")
                    for s in range(NS):
                        def ev(ps, bps, s=s):
                            S.op("act", lambda e: e.activation(out=tm_b[:, s, :], in_=ps[:, :], func=AF.Silu),
                                 reads=[bps], writes=[b_tmb])
                        proj_tm(wt, bw, uT, b_uT, s, 0, 512, ev)

            def s_tr(g):
                gi = g % 2
                for s in range(NS):
                    pt, po, bpt = psumT()
                    for fb in range(4):
                        S.op("pe", lambda e: e.transpose(out=pt[:, po + fb, :], in_=xsT[:, fb, s * 128:(s + 1) * 128],
                                                         identity=ident[:]),
                             reads=[b_xsT, b_id], writes=[bpt], signal=(fb == 3))
                    S.op("act", lambda e: e.copy(out=xstok[:, s, :].rearrange("p (a b) -> p a b", a=4),
                                                 in_=pt[:, po:po + 4, :]), reads=[bpt], writes=[b_xstok])
                    pt, po, bpt = psumT()
                    S.op("pe", lambda e: e.transpose(out=pt[:, po, :], in_=bcT[:, gi * 2, s * 128:(s + 1) * 128],
                                                     identity=ident[:]), reads=[b_bcT, b_id], writes=[bpt])
                    S.op("act", lambda e: e.copy(out=Btok[:, s, :], in_=pt[:, po, :]), reads=[bpt], writes=[b_Btok])

            def emit_Rm(g):
                hs = slice(g * 8, g * 8 + 8)
                for s in range(NS):
                    Rm, b_Rm = T[f"Rm{s}"]
                    S.op("dve", lambda e: e.tensor_tensor(
                        out=Rm[:, :].rearrange("p (a b) -> p a b", a=8),
                        in0=dtA[:, s, hs].unsqueeze(2).to_broadcast([128, 8, 128]),
                        in1=tri[:].unsqueeze(1).to_broadcast([128, 8, 128]), op=ALU.mult),
                        reads=[b_dtA, b_tri], writes=[b_Rm])

            def s_scan(g):
                if True:
                    gi = g % 2
                    hs = slice(g * 8, g * 8 + 8)
                    pas = {}
                    for s in range(NS):
                        Rm, b_Rm = T[f"Rm{s}"]
                        xdt, b_xdt = T[f"xdt{s}"]
                        xsd, b_xsd = T[f"xsd{s}"]
                        pa0, bpa0 = psum()
                        pa1, bpa1 = psum()
                        S.op("pe", lambda e: e.matmul(pa0[:, :], lhsT=ones[:], rhs=Rm[:, 0:512], start=True, stop=True),
                             reads=[b_ones, b_Rm], writes=[bpa0])
                        S.op("pe", lambda e: e.matmul(pa1[:, :], lhsT=ones[:], rhs=Rm[:, 512:1024], start=True, stop=True),
                             reads=[b_ones, b_Rm], writes=[bpa1])
                        pas[s] = ((pa0, bpa0), (pa1, bpa1))
                        S.op("dve", lambda e: e.tensor_tensor(
                            out=xdt[:].rearrange("p (a b) -> p a b", a=8),
                            in0=xstok[:, s, :].rearrange("p (a b) -> p a b", a=8),
                            in1=dtv[:, s, hs].unsqueeze(2).to_broadcast([128, 8, 64]), op=ALU.mult),
                            reads=[b_xstok, b_dtv], writes=[b_xdt])
                        if not so:
                            S.op("dve", lambda e: e.tensor_tensor(
                                out=xsd[:].rearrange("p (a b) -> p a b", a=8),
                                in0=xstok[:, s, :].rearrange("p (a b) -> p a b", a=8),
                                in1=hv[:, 2, hs].unsqueeze(2).to_broadcast([128, 8, 64]), op=ALU.mult),
                                reads=[b_xstok, b_const], writes=[b_xsd])
                    if g < 7:
                        emit_Rm(g + 1)
                    for s in range(NS):
                        dmat, b_dmat = T[f"dmat{s}"]
                        scT, b_scT = T[f"scT{s}"]
                        cbm, b_cbm = T[f"cbm{s}"]
                        xdt, b_xdt = T[f"xdt{s}"]
                        xw, b_xw = T[f"xw{s}"]
                        wst, b_wst = T[f"wst{s}"]
                        ela, b_ela = T[f"ela{s}"]
                        sl = slice(s * 128, (s + 1) * 128)
                        BT = bcT[:, gi * 2, sl]
                        CT = bcT[:, gi * 2 + 1, sl]
                        for hf, (pa, bpa) in enumerate(pas[s]):
                            S.op("dve", lambda e: e.tensor_copy(
                                out=ela[:, hf * 4:(hf + 1) * 4],
                                in_=pa[:, :].rearrange("p (a b) -> p a b", a=4)[:, :, 127]),
                                reads=[bpa], writes=[b_ela])
                            if not so:
                                for q in range(4):
                                    hh = hf * 4 + q
                                    S.op("act", lambda e: e.activation(out=dmat[:, hh, :], in_=pa[:, q * 128:(q + 1) * 128],
                                                                       func=AF.Relu, scale=-1.0,
                                                                       bias=acum[:, s, g * 8 + hh:g * 8 + hh + 1]),
                                         reads=[bpa, b_acum], writes=[b_dmat])
                        if not so:
                            S.op("act", lambda e: e.activation(out=dmat[:], in_=dmat[:], func=AF.Exp, scale=-1.0),
                                 reads=[b_dmat], writes=[b_dmat])
                        S.op("dve", lambda e: e.tensor_tensor(out=wst[:], in0=ela[:], in1=acum[:, s, hs], op=ALU.subtract),
                             reads=[b_ela, b_acum], writes=[b_wst])
                        S.op("act", lambda e: e.activation(out=wst[:], in_=wst[:], func=AF.Exp), reads=[b_wst],
                             writes=[b_wst])
                        S.op("act", lambda e: e.activation(out=ela[:], in_=ela[:], func=AF.Exp), reads=[b_ela],
                             writes=[b_ela])
                        S.op("dve", lambda e: e.tensor_tensor(
                            out=xw[:].rearrange("p (a b) -> p a b", a=8), in0=xdt[:].rearrange("p (a b) -> p a b", a=8),
                            in1=wst[:].unsqueeze(2).to_broadcast([128, 8, 64]), op=ALU.mult),
                            reads=[b_xdt, b_wst], writes=[b_xw])
                        if not so:
                            pcb, bpcb = psum()
                            S.op("pe", lambda e: e.matmul(pcb[:, 0:128], lhsT=BT, rhs=CT, start=True, stop=True),
                                 reads=[b_bcT], writes=[bpcb])
                            S.op("dve", lambda e: e.tensor_tensor(out=cbm[:], in0=pcb[:, 0:128], in1=tri[:], op=ALU.mult),
                                 reads=[bpcb, b_tri], writes=[b_cbm])
                            S.op("dve", lambda e: e.tensor_tensor(
                                out=scT[:], in0=dmat[:], in1=cbm[:].unsqueeze(1).to_broadcast([128, 8, 128]),
                                op=ALU.mult), reads=[b_dmat, b_cbm], writes=[b_scT])
                    for s in range(NS):
                        scT, b_scT = T[f"scT{s}"]
                        xdt, b_xdt = T[f"xdt{s}"]
                        xsd, b_xsd = T[f"xsd{s}"]
                        xw, b_xw = T[f"xw{s}"]
                        ela, b_ela = T[f"ela{s}"]
                        sl = slice(s * 128, (s + 1) * 128)
                        CT = bcT[:, gi * 2 + 1, sl]
                        if not so:
                            S.op("act", lambda e: e.copy(out=Stb[:], in_=St[:, g, :]), reads=[b_St], writes=[b_Stb])
                        S.op("dve", lambda e: e.tensor_tensor(
                            out=St[:, g, :].rearrange("p (a b) -> p a b", a=8),
                            in0=St[:, g, :].rearrange("p (a b) -> p a b", a=8),
                            in1=ela[:].unsqueeze(2).to_broadcast([128, 8, 64]), op=ALU.mult),
                            reads=[b_St, b_ela], writes=[b_St])
                        if not so:
                            py, bpy = psum()
                            for hh in range(8):
                                S.op("pe", lambda e: e.matmul(py[:, hh * 64:(hh + 1) * 64], lhsT=scT[:, hh, :],
                                                              rhs=xdt[:, hh * 64:(hh + 1) * 64], start=True, stop=False),
                                     reads=[b_scT, b_xdt], writes=[bpy], signal=False)
                                S.op("pe", lambda e: e.matmul(py[:, hh * 64:(hh + 1) * 64], lhsT=ident[:],
                                                              rhs=xsd[:, hh * 64:(hh + 1) * 64], start=False, stop=True),
                                     reads=[b_id, b_xsd], writes=[bpy], signal=(hh == 7))
                            pyi, bpyi = psum()
                            S.op("pe", lambda e: e.matmul(pyi[:, :], lhsT=CT, rhs=Stb[:], start=True, stop=True),
                                 reads=[b_bcT, b_Stb], writes=[bpyi])
                        pd, bpd = psum()
                        S.op("pe", lambda e: e.matmul(pd[:, :], lhsT=Btok[:, s, :], rhs=xw[:], start=True, stop=True),
                             reads=[b_Btok, b_xw], writes=[bpd])
                        if not so:
                            S.op("dve", lambda e: e.tensor_tensor(
                                out=ytmp[:].rearrange("p (a b) -> p a b", a=8), in0=pyi[:, :].rearrange("p (a b) -> p a b", a=8),
                                in1=expa[:, s, hs].unsqueeze(2).to_broadcast([128, 8, 64]), op=ALU.mult),
                                reads=[bpyi, b_expa], writes=[b_ytmp])
                            S.op("dve", lambda e: e.tensor_tensor(out=ytmp[:], in0=ytmp[:], in1=py[:, :], op=ALU.add),
                                 reads=[bpy, b_ytmp], writes=[b_ytmp])
                        S.op("dve", lambda e: e.tensor_tensor(out=St[:, g, :], in0=St[:, g, :], in1=pd[:, :], op=ALU.add),
                             reads=[b_St, bpd], writes=[b_St])
                        if not so:
                            S.op("dve", lambda e: e.tensor_tensor(out=tm_a[:, s, :], in0=ytmp[:], in1=tm_b[:, s, :],
                                                                   op=ALU.mult), reads=[b_ytmp, b_tmb], writes=[b_tma])
                            S.op("act", lambda e: e.activation(out=pjunk[:], in_=tm_a[:, s, :], func=AF.Square,
                                                               accum_out=ss[:, s:s + 1]), reads=[b_tma], writes=[b_pjunk, b_ss])

            def s_post(g):
                if True:
                    aT, baT = actT[g // 4]
                    if not so:
                        S.op("act", lambda e: e.activation(out=ss[:, 0:NS], in_=ss[:, 0:NS], func=AF.Ln, scale=1.0 / 512,
                                                           bias=EPS), reads=[b_ss], writes=[b_ss])
                        S.op("act", lambda e: e.activation(out=ss[:, 0:NS], in_=ss[:, 0:NS], func=AF.Exp, scale=-0.5),
                             reads=[b_ss], writes=[b_ss])
                        for s in range(NS):
                            S.op("dve", lambda e: e.tensor_scalar(out=onb[:, s, :], in0=tm_a[:, s, :], scalar1=ss[:, s:s + 1],
                                                                  scalar2=None, op0=ALU.mult),
                                 reads=[b_tma, b_ss], writes=[b_onb])
                            pass

            def s_post_b(g):
                aT, baT = actT[g // 4]
                if not so:
                    for s in range(NS):
                        transpose_to_actT(onb, b_onb, s, aT, baT, (g % 4) * 4, gnw[:, g * 4:(g + 1) * 4])
            emit_Rm(0)
            s_proj(0)
            s_tr(0)
            for g in range(8):
                s_scan(g)
                s_post(g)
                if g < 7:
                    s_proj(g + 1)
                s_post_b(g)
                if g < 7:
                    s_tr(g + 1)
            pes.close()
            if not so:
                for kb in range(2):
                    for nb in range(4):
                        outproj("sout", actT[kb][0], actT[kb][1], nb)

        if os_environ_get('KINFO'):
            print('SBUF bytes remaining/partition:', nc.sbuf_bytes_remaining)
        nwidx = {0: (0, 1), 1: (2, 3)}
        import os

        class _Stop(Exception):
            pass

        def ck(name):
            if os.environ.get("KSTOP") == name:
                raise _Stop()
        build_program.ck = ck
        try:
          for t in range(NT):
              is_main = t >= n_pre
              S.dma("sp", d_x, hres[:], x_d[t * TT:(t + 1) * TT, :].rearrange("(s p) d -> p s d", p=128),
                    writes=[b_hres])
              ck("load")
              for L in layers:
                  norm_to_uT(nwidx[L][0])
                  ck("norm")
                  pre_so = (not is_main) and L == 1
                  if L == 0:
                      gla()
                  else:
                      ssd(state_only=pre_so)
                  if not pre_so:
                      norm_to_uT(nwidx[L][1])
                      mlp()
              if t == n_pre - 1:
                  for (tt_, bb_) in ((Sg, b_Sg), (St, b_St), (cs, b_cs)):
                      flat = tt_[:].rearrange("p a b c -> p (a b c)") if tt_ is Sg else tt_[:].rearrange("p a b -> p (a b)")
                      S.op("dve", lambda e: e.tensor_scalar(out=flat, in0=flat, scalar1=flag[:, 0:1], scalar2=None,
                                                            op0=ALU.mult), reads=[bb_, b_const], writes=[bb_])
              if is_main:
                  tm = t - n_pre
                  for s in range(NS):
                      if final_norm:
                          S.op("act", lambda e: e.activation(out=ubf[:, 0, :], in_=hres[:, s, :], func=AF.Square,
                                                             accum_out=ss[:, s:s + 1]), reads=[b_hres], writes=b_ubfp[0] + [b_ss])
                          S.op("act", lambda e: e.activation(out=ss[:, s:s + 1], in_=ss[:, s:s + 1], func=AF.Ln, scale=1.0 / D,
                                                             bias=EPS), reads=[b_ss], writes=[b_ss])
                          S.op("act", lambda e: e.activation(out=ss[:, s:s + 1], in_=ss[:, s:s + 1], func=AF.Exp, scale=-0.5),
                               reads=[b_ss], writes=[b_ss])
                          S.op("dve", lambda e: e.scalar_tensor_tensor(out=hres[:, s, :], in0=hres[:, s, :], scalar=ss[:, s:s + 1],
                                                                       in1=fnw[:], op0=ALU.mult, op1=ALU.mult),
                               reads=[b_hres, b_ss, b_const], writes=[b_hres])
                          S.dma("sp", d_out, out_d[tm * TT + s * 128: tm * TT + (s + 1) * 128, :], hres[:, s, :],
                                reads=[b_hres])
                      else:
                          S.dma("sp", d_out, out_d[tm * TT + s * 128: tm * TT + (s + 1) * 128, :], hres[:, s, :],
                                reads=[b_hres])
        except _Stop:
            pass
        else:
            assert wstate["next"] == wstate["total"] == wstate["issued"]
        S.wait_all("sp", [d_out])
        S.wait_all("sp", ["pe", "dve", "act", "pool"] + d_w)
    return nc


def small_inputs(inputs):
    f = np.float32
    fm = lambda v: np.ascontiguousarray(v.reshape(-1, 128).T).astype(f)
    normw = np.concatenate([fm(inputs["mixer_norm_w"][0]), fm(inputs["mlp_norm_w"][0]),
                            fm(inputs["mixer_norm_w"][1]), fm(inputs["mlp_norm_w"][1])], axis=1)
    cwf = inputs["ssd_conv_w"][0]
    cbf = inputs["ssd_conv_b"][0]
    blocks = []
    for gp in range(4):
        g0, g1 = 2 * gp, 2 * gp + 1
        for base in (4096 + g0 * 128, 5120 + g0 * 128, 4096 + g1 * 128, 5120 + g1 * 128):
            blocks.append(np.arange(base, base + 128))
    for g in range(8):
        for fb in range(4):
            blocks.append(np.arange(g * 512 + fb * 128, g * 512 + fb * 128 + 128))
    convw = np.stack([cwf[:, b].T for b in blocks], axis=1)
    convb = np.stack([cbf[b] for b in blocks], axis=1)
    return {
        "normw": np.ascontiguousarray(normw, dtype=f),
        "fnw": inputs["final_norm_w"].reshape(1, D).astype(f),
        "onw": fm(inputs["gla_o_norm_w"][0]),
        "gnw": fm(inputs["ssd_gnorm_w"][0]),
        "bgk": inputs["gla_b_gk_up"][0].reshape(1, 1024).astype(f),
        "wgk": np.ascontiguousarray(inputs["gla_w_gk_up"][0], dtype=f),
        "headvec": np.concatenate([inputs["ssd_dt_bias"][0], inputs["ssd_a_log"][0],
                                   inputs["ssd_d_skip"][0]]).reshape(1, 192).astype(f),
        "convw": np.ascontiguousarray(convw.reshape(128, 48 * 4), dtype=f),
        "convb": np.ascontiguousarray(convb, dtype=f),
    }


def run_stage(inputs, xs_per_core, layers, n_pre, n_main, TT, final_norm, flags):
    wstream = build_wstream(inputs, layers)
    nc = build_program(layers, n_pre, n_main, TT, final_norm, int(wstream.size))
    small = small_inputs(inputs)
    in_maps = []
    for c, xc in enumerate(xs_per_core):
        m = dict(small)
        m["x"] = np.ascontiguousarray(xc, dtype=np.float32)
        m["wstream"] = wstream
        m["flag"] = np.full((128, 1), flags[c], np.float32)
        in_maps.append(m)
    res = run_bass_kernel_spmd(nc, in_maps, core_ids=list(range(len(xs_per_core))))
    return [r["out"] for r in res.results]


TT_DEFAULT = 256
FUSED = True


def kernel(**inputs):
    inputs = {k: np.asarray(v) for k, v in inputs.items()}
    x = inputs["x"]
    TT = TT_DEFAULT
    npre = HALF // TT
    nmain = HALF // TT
    flags = [float(c % 2) for c in range(NCORES)]

    def shard(h):
        xs = []
        for c in range(NCORES):
            b, half = c // 2, c % 2
            first = h[b, 0:HALF] if half == 1 else np.zeros((HALF, D), np.float32)
            mine = h[b, half * HALF:(half + 1) * HALF]
            xs.append(np.concatenate([first, mine], axis=0))
        return xs

    def gather(outs):
        o = np.empty((BATCH, SEQ, D), np.float32)
        for c in range(NCORES):
            b, half = c // 2, c % 2
            o[b, half * HALF:(half + 1) * HALF] = outs[c]
        return o

    if FUSED:
        return gather(run_stage(inputs, shard(x), [0, 1], npre, nmain, TT, True, flags))
    h1 = gather(run_stage(inputs, shard(x), [0], npre, nmain, TT, False, flags))
    return gather(run_stage(inputs, shard(h1), [1], npre, nmain, TT, True, flags))
```

```python
import numpy as np
import os
os_environ_get = os.environ.get
from contextlib import ExitStack
import concourse.bass as bass
import concourse.mybir as mybir
from concourse.bass_utils import run_bass_kernel_spmd

F32 = mybir.dt.float32
BF16 = mybir.dt.bfloat16
ALU = mybir.AluOpType
AF = mybir.ActivationFunctionType

D = 2048
NCORES = 8
SEQ = 4096
BATCH = 4
HALF = SEQ // 2
EPS = 1e-5


class Buf:
    __slots__ = ("name", "w", "r", "excl")

    def __init__(self, name, excl=False):
        self.name = name
        self.w = None
        self.r = {}
        self.excl = excl


class Sched:
    def __init__(self, nc, es, same_eng_sync=True):
        self.nc = nc
        self.es = es
        self.engs = {"pe": nc.tensor, "dve": nc.vector, "act": nc.scalar,
                     "pool": nc.gpsimd, "sp": nc.sync}
        self.sems = {}
        self.cnt = {}
        for k in self.engs:
            self.sems[k] = es.enter_context(nc.semaphore("sem_" + k))
            self.cnt[k] = 0
        self.waited = {k: {} for k in self.engs}
        self.same = same_eng_sync
        self.nops = 0

    def new_dma_sem(self, name):
        key = "dma_" + name
        self.sems[key] = self.es.enter_context(self.nc.semaphore(key))
        self.cnt[key] = 0
        return key

    def _deps(self, reads, writes, e=None):
        deps = {}
        for b in reads:
            if b.w is not None:
                k, v = b.w
                if deps.get(k, 0) < v:
                    deps[k] = v
            if b.excl:
                for k, v in b.r.items():
                    if k != e and deps.get(k, 0) < v:
                        deps[k] = v
        for b in writes:
            if b.w is not None:
                k, v = b.w
                if deps.get(k, 0) < v:
                    deps[k] = v
            for k, v in b.r.items():
                if deps.get(k, 0) < v:
                    deps[k] = v
        return deps

    def _wait(self, e, deps):
        for k, v in deps.items():
            if k == e and (not self.same or e == "pe"):
                continue
            if self.waited[e].get(k, 0) >= v:
                continue
            self.engs[e].wait_ge(self.sems[k], v)
            self.waited[e][k] = v

    def op(self, e, fn, reads=(), writes=(), signal=True):
        self._wait(e, self._deps(reads, writes, e))
        ins = fn(self.engs[e])
        self.nops += 1
        if signal:
            self.cnt[e] += 1
            ins.then_inc(self.sems[e], 1)
            v = self.cnt[e]
        else:
            v = self.cnt[e] + 1
        for b in reads:
            if b.r.get(e, 0) < v:
                b.r[e] = v
        for b in writes:
            b.w = (e, v)
            b.r = {}
        return ins

    def dma(self, q, dkey, out_ap, in_ap, reads=(), writes=(), **kw):
        self._wait(q, self._deps(reads, writes))
        ins = self.engs[q].dma_start(out=out_ap, in_=in_ap, **kw)
        self.cnt[dkey] += 16
        ins.then_inc(self.sems[dkey], 16)
        v = self.cnt[dkey]
        for b in reads:
            b.r[dkey] = v
        for b in writes:
            b.w = (dkey, v)
            b.r = {}
        return ins

    def wait_all(self, e, keys):
        for k in keys:
            v = self.cnt[k]
            if v > 0 and self.waited[e].get(k, 0) < v:
                self.engs[e].wait_ge(self.sems[k], v)
                self.waited[e][k] = v


def tile_specs(layers):
    sp = []
    ar = np.arange

    def mlp(i):
        for kb in range(4):
            for j in range(4):
                c0 = (4 * kb + j) * 512
                sp.append(("fc1", "mlp_w_fc1", i, 0, ar(c0, c0 + 512)))
            for nb in range(4):
                sp.append(("fc2", "mlp_w_fc2", i, kb * 2048, ar(nb * 512, nb * 512 + 512)))

    for L in layers:
        if L == 0:
            sp.append(("gr", "gla_w_in", 0, 0, ar(6144, 6160)))
            for h in range(4):
                sp.append(("qk", "gla_w_in", 0, 0,
                           np.concatenate([ar(h * 256, h * 256 + 256), ar(1024 + h * 256, 1024 + h * 256 + 256)])))
                sp.append(("v", "gla_w_in", 0, 0, ar(2048 + h * 512, 2048 + h * 512 + 512)))
                sp.append(("g", "gla_w_in", 0, 0, ar(4096 + h * 512, 4096 + h * 512 + 512)))
            for nb in range(4):
                sp.append(("gout", "gla_w_out", 0, 0, ar(nb * 512, nb * 512 + 512)))
            mlp(0)
        else:
            sp.append(("dt", "ssd_w_in", 0, 0, ar(10240, 10304)))
            for gp in range(4):
                g0, g1 = 2 * gp, 2 * gp + 1
                sp.append(("bc", "ssd_w_in", 0, 0, np.concatenate([
                    ar(8192 + g0 * 128, 8192 + g0 * 128 + 128), ar(9216 + g0 * 128, 9216 + g0 * 128 + 128),
                    ar(8192 + g1 * 128, 8192 + g1 * 128 + 128), ar(9216 + g1 * 128, 9216 + g1 * 128 + 128)])))
                for g in (g0, g1):
                    sp.append(("xs", "ssd_w_in", 0, 0, ar(4096 + g * 512, 4096 + g * 512 + 512)))
                    sp.append(("z", "ssd_w_in", 0, 0, ar(g * 512, g * 512 + 512)))
            for kb in range(2):
                for nb in range(4):
                    sp.append(("sout", "ssd_w_out", 0, kb * 2048, ar(nb * 512, nb * 512 + 512)))
            mlp(1)
    return sp


def build_wstream(inputs, layers):
    parts = []
    for (_, key, slot, r0, cols) in tile_specs(layers):
        w = inputs[key][slot]
        blk = w[r0:r0 + 2048][:, cols]
        parts.append(np.ascontiguousarray(
            blk.reshape(16, 128, len(cols)).transpose(1, 0, 2)).reshape(-1))
    return np.concatenate(parts).astype(np.float32, copy=False)


def build_program(layers, n_pre, n_main, TT, final_norm, wtotal, nwb=3):
    NS = TT // 128
    NT = n_pre + n_main
    nc = bass.Bass("TRN2", target_bir_lowering=False)
    specs = tile_specs(layers)
    x_d = nc.dram_tensor("x", [NT * TT, D], F32, kind="ExternalInput").ap()
    w_d = nc.dram_tensor("wstream", [wtotal], F32, kind="ExternalInput").ap()
    normw_d = nc.dram_tensor("normw", [128, 4 * 16], F32, kind="ExternalInput").ap()
    fnw_d = nc.dram_tensor("fnw", [1, D], F32, kind="ExternalInput").ap()
    onw_d = nc.dram_tensor("onw", [128, 4], F32, kind="ExternalInput").ap()
    gnw_d = nc.dram_tensor("gnw", [128, 32], F32, kind="ExternalInput").ap()
    bgk_d = nc.dram_tensor("bgk", [1, 1024], F32, kind="ExternalInput").ap()
    wgk_d = nc.dram_tensor("wgk", [16, 1024], F32, kind="ExternalInput").ap()
    hv_d = nc.dram_tensor("headvec", [1, 3 * 64], F32, kind="ExternalInput").ap()
    cw_d = nc.dram_tensor("convw", [128, 48 * 4], F32, kind="ExternalInput").ap()
    cb_d = nc.dram_tensor("convb", [128, 48], F32, kind="ExternalInput").ap()
    flag_d = nc.dram_tensor("flag", [128, 1], F32, kind="ExternalInput").ap()
    out_d = nc.dram_tensor("out", [n_main * TT, D], F32, kind="ExternalOutput").ap()
    wbf_d = nc.dram_tensor("wbf16", [wtotal], BF16, kind="Internal").ap()

    es = ExitStack()
    with es:
        es.enter_context(nc.allow_low_precision("bf16 matmul operands, fp32 accumulation"))
        S = Sched(nc, es)

        def sb(name, shape, dt):
            return es.enter_context(nc.sbuf_tensor("sb_" + name, shape, dt)), Buf(name)

        hres, b_hres = sb("hres", [128, NS, D], F32)
        uT, b_uT = sb("uT", [128, 16, TT], BF16)
        wbufs = [sb(f"wb{i}", [128, 16, 512], BF16) for i in range(nwb)]
        actT = [sb(f"actT{i}", [128, 16, TT], BF16) for i in range(2)]
        tm_a, b_tma = sb("tm_a", [128, NS, 512], F32)
        tm_b, b_tmb = sb("tm_b", [128, NS, 512], F32)
        ubf, b_ubf = sb("ubf", [128, NS, D], BF16)
        b_ubfp = [[Buf(f"ubfp{j}_{i}") for i in range(4)] for j in range(NS)]
        nss, _ = sb("nss", [128, NS], F32)
        b_nss = [Buf(f"nss{i}") for i in range(NS)]
        ss, b_ss = sb("ss", [128, 8], F32)
        ident, b_id = sb("ident", [128, 128], BF16)
        tri, b_tri = sb("tri", [128, 128], F32)
        triU, b_triU = sb("triU", [128, 128], F32)
        ones, b_ones = sb("ones", [128, 128], F32)
        normw, b_normw = sb("normw", [128, 4, 16], F32)
        fnw, b_fnw = sb("fnw", [128, D], F32)
        onw, b_onw = sb("onw", [128, 4], F32)
        gnw, b_gnw = sb("gnw", [128, 32], F32)
        bgk, b_bgk = sb("bgk", [128, 1024], F32)
        wgk, b_wgk = sb("wgk", [16, 1024], BF16)
        hv, b_hv = sb("hv", [128, 3, 64], F32)
        negA, b_negA = sb("negA", [128, 64], F32)
        cw, b_cw = sb("cw", [128, 48, 4], F32)
        cb, b_cb = sb("cb", [128, 48], F32)
        flag, b_flag = sb("flag", [128, 1], F32)
        Sg, b_Sg = sb("Sg", [128, 4, 2, 512], F32)
        St, b_St = sb("St", [128, 8, 512], F32)
        cs, b_cs = sb("cs", [128, 48, 3], F32)
        onb, b_onb = sb("onb", [128, NS, 512], BF16)
        rtmp, b_rtmp = sb("rtmp", [128, TT], F32)
        phase_ctr = [0]
        prev_phase_bufs = []

        def phase_alloc(pes, decls):
            phase_ctr[0] += 1
            out = {}
            for name, shape, dt in decls:
                t = pes.enter_context(nc.sbuf_tensor(f"ph{phase_ctr[0]}_{name}", shape, dt))
                out[name] = (t, Buf(name))
            for e in ("pe", "dve", "act", "pool"):
                S._wait(e, S._deps([], prev_phase_bufs, e))
            prev_phase_bufs[:] = [bb for (_, bb) in out.values()]
            return out

        NPS = 6
        psb = []
        for i in range(NPS):
            t = es.enter_context(nc.psum_tensor(f"ps{i}", [128, 512], F32))
            psb.append((t, Buf(f"ps{i}", excl=True)))
        pT = []
        for i in range(2):
            t = es.enter_context(nc.psum_tensor(f"psT{i}", [128, 8, 128], BF16))
            pT.append((t, 0, Buf(f"psT{i}", excl=True)))
        ps_i = [0]
        pt_i = [0]

        def psum():
            t, b = psb[ps_i[0] % NPS]
            ps_i[0] += 1
            return t, b

        def psumT():
            r = pT[pt_i[0] % 2]
            pt_i[0] += 1
            return r

        d_const = S.new_dma_sem("const")
        d_x = S.new_dma_sem("x")
        d_out = S.new_dma_sem("out")
        d_w = [S.new_dma_sem(f"w{i}") for i in range(nwb)]

        b_const = Buf("consts")
        cl = [(normw[:].rearrange("p a b -> p (a b)"), normw_d), (fnw[:], fnw_d.to_broadcast([128, D])),
              (onw[:], onw_d), (gnw[:], gnw_d), (bgk[:], bgk_d.to_broadcast([128, 1024])),
              (hv[:].rearrange("p a b -> p (a b)"), hv_d.to_broadcast([128, 192])),
              (cw[:].rearrange("p a b -> p (a b)"), cw_d), (cb[:], cb_d), (flag[:], flag_d)]
        for o_ap, i_ap in cl:
            S.dma("sp", d_const, o_ap, i_ap, writes=[b_const])
        b_wgk2 = Buf("wgk2")
        S.dma("pool", S.new_dma_sem("wgk"), wgk[:], wgk_d, writes=[b_wgk2])
        S.op("pool", lambda e: e.memset(ident[:], 1.0), writes=[b_id])
        S.op("pool", lambda e: e.affine_select(out=ident[:], in_=ident[:], pattern=[[-1, 128]], compare_op=ALU.is_equal,
                                               fill=0.0, base=0, channel_multiplier=1), reads=[b_id], writes=[b_id])
        S.op("pool", lambda e: e.memset(tri[:], 1.0), writes=[b_tri])
        S.op("pool", lambda e: e.affine_select(out=tri[:], in_=tri[:], pattern=[[1, 128]], compare_op=ALU.is_ge,
                                               fill=0.0, base=0, channel_multiplier=-1), reads=[b_tri], writes=[b_tri])
        S.op("pool", lambda e: e.memset(triU[:], 1.0), writes=[b_triU])
        S.op("pool", lambda e: e.affine_select(out=triU[:], in_=triU[:], pattern=[[-1, 128]], compare_op=ALU.is_gt,
                                               fill=0.0, base=0, channel_multiplier=1), reads=[b_triU], writes=[b_triU])
        S.op("pool", lambda e: e.memset(ones[:], 1.0), writes=[b_ones])
        S.op("pool", lambda e: e.memset(Sg[:], 0.0), writes=[b_Sg])
        S.op("pool", lambda e: e.memset(St[:], 0.0), writes=[b_St])
        S.op("pool", lambda e: e.memset(cs[:], 0.0), writes=[b_cs])
        S.op("act", lambda e: e.activation(out=negA[:], in_=hv[:, 1, :], func=AF.Exp), reads=[b_const], writes=[b_negA])
        S.op("dve", lambda e: e.tensor_scalar(out=negA[:], in0=negA[:], scalar1=-1.0, scalar2=None, op0=ALU.mult),
             reads=[b_negA], writes=[b_negA])
        CONST = [b_const, b_id, b_tri, b_triU, b_ones, b_negA]

        offs = []
        o = 0
        for sp_ in specs:
            offs.append(o)
            o += 2048 * len(sp_[4])
        assert o == wtotal, (o, wtotal)
        def skip_in_prefix(sp_):
            return (sp_[0] in ("z", "sout")) or (sp_[0] in ("fc1", "fc2") and sp_[2] == 1)
        seq = []
        for t in range(NT):
            for i, sp_ in enumerate(specs):
                if t < n_pre and skip_in_prefix(sp_):
                    continue
                seq.append(i)
        wstate = {"issued": 0, "next": 0, "total": len(seq)}
        d_ws = [S.new_dma_sem(f"ws{i}") for i in range(nwb)]
        b_scr = [Buf(f"scr{i}") for i in range(len(specs))]
        converted = [False] * len(specs)

        def issue_w(n):
            t = seq[n]
            ncols = len(specs[t][4])
            wt, bw = wbufs[n % nwb]
            if not converted[t]:
                src = w_d[offs[t]: offs[t] + 2048 * ncols].rearrange("(p k n) -> p k n", p=128, k=16)
                S.dma("pool", d_w[n % nwb], wt[:, :, 0:ncols], src, writes=[bw])
                if NT > 1:
                    dst = wbf_d[offs[t]: offs[t] + 2048 * ncols].rearrange("(p k n) -> p k n", p=128, k=16)
                    S.dma("sp", d_ws[n % nwb], dst, wt[:, :, 0:ncols], reads=[bw], writes=[b_scr[t]])
                converted[t] = True
            else:
                src = wbf_d[offs[t]: offs[t] + 2048 * ncols].rearrange("(p k n) -> p k n", p=128, k=16)
                S.dma("sp", d_wl[n % nwb], wt[:, :, 0:ncols], src, reads=[b_scr[t]], writes=[bw])

        d_wl = [S.new_dma_sem(f"wl{i}") for i in range(nwb)]

        def next_tile(kind):
            n = wstate["next"]
            assert specs[seq[n]][0] == kind, (specs[seq[n]][0], kind)
            while wstate["issued"] < min(n + nwb, wstate["total"]):
                issue_w(wstate["issued"])
                wstate["issued"] += 1
            wstate["next"] += 1
            return wbufs[n % nwb]

        def rstd_from_ss(n):
            pass

        def norm_to_uT(widx):
            for s in range(NS):
                S.op("act", lambda e: e.activation(out=ubf[:, s, :], in_=hres[:, s, :], func=AF.Square,
                                                   accum_out=nss[:, s:s + 1]),
                     reads=[b_hres], writes=b_ubfp[s] + [b_nss[s]])
                S.op("act", lambda e: e.activation(out=nss[:, s:s + 1], in_=nss[:, s:s + 1], func=AF.Ln, scale=1.0 / D,
                                                   bias=EPS), reads=[b_nss[s]], writes=[b_nss[s]])
                S.op("act", lambda e: e.activation(out=nss[:, s:s + 1], in_=nss[:, s:s + 1], func=AF.Exp, scale=-0.5),
                     reads=[b_nss[s]], writes=[b_nss[s]])
            def piece(k):
                s, grp = divmod(k, 4)
                S.op("dve",
                     lambda e: e.tensor_scalar(out=ubf[:, s, grp * 512:(grp + 1) * 512],
                                               in0=hres[:, s, grp * 512:(grp + 1) * 512],
                                               scalar1=nss[:, s:s + 1], scalar2=None, op0=ALU.mult),
                     reads=[b_hres, b_nss[s]], writes=[b_ubfp[s][grp]])
            NG = NS * 4
            piece(0)
            piece(1)
            for k in range(NG):
                s, grp = divmod(k, 4)
                pt, po, bpt = psumT()
                for j in range(4):
                    kc = grp * 4 + j
                    S.op("pe", lambda e: e.transpose(out=pt[:, po + j, :], in_=ubf[:, s, kc * 128:(kc + 1) * 128],
                                                     identity=ident[:]),
                         reads=[b_ubfp[s][grp], b_id], writes=[bpt], signal=(j == 3))
                if k + 2 < NG:
                    piece(k + 2)
                if k % 2 == 0:
                    S.op("dve", lambda e: e.tensor_tensor(
                        out=uT[:, grp * 4:(grp + 1) * 4, s * 128:(s + 1) * 128], in0=pt[:, po:po + 4, :],
                        in1=normw[:, widx, grp * 4:(grp + 1) * 4].unsqueeze(2).to_broadcast([128, 4, 128]),
                        op=ALU.mult), reads=[bpt, b_const], writes=[b_uT])
                else:
                    for j in range(4):
                        kc = grp * 4 + j
                        S.op("act", lambda e: e.mul(out=uT[:, kc, s * 128:(s + 1) * 128], in_=pt[:, po + j, :],
                                                    mul=normw[:, widx, kc:kc + 1]),
                             reads=[bpt, b_const], writes=[b_uT])

        def proj_fm(wt, bw, c0, ncol, evac):
            ps, bps = psum()
            for kc in range(16):
                S.op("pe", lambda e: e.matmul(ps[0:ncol, 0:TT], lhsT=wt[:, kc, c0:c0 + ncol], rhs=uT[:, kc, :],
                                              start=(kc == 0), stop=(kc == 15)),
                     reads=[bw, b_uT], writes=[bps], signal=(kc == 15))
            evac(ps, bps)

        def proj_tm(wt, bw, src, bsrc, s, c0, ncol, evac):
            ps, bps = psum()
            for kc in range(16):
                S.op("pe", lambda e: e.matmul(ps[:, 0:ncol], lhsT=src[:, kc, s * 128:(s + 1) * 128],
                                              rhs=wt[:, kc, c0:c0 + ncol], start=(kc == 0), stop=(kc == 15)),
                     reads=[bw, bsrc], writes=[bps], signal=(kc == 15))
            evac(ps, bps)

        def outproj(kind, src, bsrc, nb):
            wt, bw = next_tile(kind)
            for s in range(NS):
                def ev(ps, bps, s=s):
                    S.op("dve", lambda e: e.tensor_tensor(out=hres[:, s, nb * 512:(nb + 1) * 512],
                                                          in0=hres[:, s, nb * 512:(nb + 1) * 512], in1=ps[:, :],
                                                          op=ALU.add),
                         reads=[bps, b_hres], writes=[b_hres])
                proj_tm(wt, bw, src, bsrc, s, 0, 512, ev)

        def mlp():
            for kb in range(4):
                at, bat = actT[kb % 2]
                for j in range(4):
                    wt, bw = next_tile("fc1")
                    for fb in range(4):
                        def ev(ps, bps, j=j, fb=fb):
                            S.op("act", lambda e: e.activation(out=rtmp[:], in_=ps[:, 0:TT], func=AF.Relu),
                                 reads=[bps], writes=[b_rtmp])
                            S.op("dve", lambda e: e.tensor_tensor(out=at[:, j * 4 + fb, :], in0=rtmp[:], in1=rtmp[:],
                                                                  op=ALU.mult),
                                 reads=[b_rtmp], writes=[bat])
                        proj_fm(wt, bw, fb * 128, 128, ev)
                for nb in range(4):
                    outproj("fc2", at, bat, nb)

        def transpose_to_actT(srcb, bsrc, s, dst, bdst, kc0, scale_ap):
            pt, po, bpt = psumT()
            for ec in range(4):
                S.op("pe", lambda e: e.transpose(out=pt[:, po + ec, :], in_=srcb[:, s, ec * 128:(ec + 1) * 128],
                                                 identity=ident[:]),
                     reads=[bsrc, b_id], writes=[bpt], signal=(ec == 3))
            S.op("dve", lambda e: e.tensor_tensor(
                out=dst[:, kc0:kc0 + 4, s * 128:(s + 1) * 128], in0=pt[:, po:po + 4, :],
                in1=scale_ap.unsqueeze(2).to_broadcast([128, 4, 128]), op=ALU.mult),
                reads=[bpt, b_const], writes=[bdst])

        def gla():
            aT, baT = actT[0]
            pes = ExitStack()
            dec = [("lsp0", [128, 1024], F32), ("lsp1", [128, 1024], F32), ("grT", [16, TT], BF16),
                   ("qT", [128, 2, TT], F32), ("kT", [128, 2, TT], F32), ("ktok", [128, NS, 256], F32),
                   ("vtok", [128, NS, 512], BF16), ("Sb", [128, 2, 512], BF16), ("tpre", [128, 512], F32),
                   ("junk", [128, 512], BF16)]
            for i in range(NS):
                dec += [(f"Eq{i}", [128, 2, 128], F32), (f"Ek{i}", [128, 2, 128], F32), (f"Ed{i}", [128, 256], F32),
                        (f"qtl{i}", [128, 2, 128], BF16), (f"ktl{i}", [128, 2, 128], BF16),
                        (f"kdec{i}", [128, 256], BF16), (f"attT{i}", [128, 128], BF16)]
            T = phase_alloc(pes, dec)
            lsps = [T["lsp0"], T["lsp1"]]
            grT, b_grT = T["grT"]
            qT, b_qT = T["qT"]
            kT, b_kT = T["kT"]
            ktok, b_ktok = T["ktok"]
            vtok, b_vtok = T["vtok"]
            Sb, b_Sb = T["Sb"]
            tpre, b_tpre = T["tpre"]
            pjunk, b_pjunk = T["junk"]
            wt, bw = next_tile("gr")

            def ev_gr(ps, bps):
                S.op("act", lambda e: e.copy(out=grT[:], in_=ps[0:16, 0:TT]), reads=[bps], writes=[b_grT])
            proj_fm(wt, bw, 0, 16, ev_gr)
            ck("gr")
            for s in range(NS):
                for hf in range(2):
                    ps, bps = psum()
                    S.op("pe", lambda e: e.matmul(ps[:, :], lhsT=grT[0:16, s * 128:(s + 1) * 128],
                                                  rhs=wgk[0:16, hf * 512:(hf + 1) * 512], start=True, stop=True),
                         reads=[b_grT, b_wgk2], writes=[bps])
                    S.op("dve", lambda e: e.tensor_tensor(out=tpre[:], in0=ps[:, :], in1=bgk[:, hf * 512:(hf + 1) * 512],
                                                          op=ALU.add), reads=[bps, b_const], writes=[b_tpre])
                    S.op("act", lambda e: e.activation(out=tpre[:], in_=tpre[:], func=AF.Exp, scale=-1.0),
                         reads=[b_tpre], writes=[b_tpre])
                    S.op("act", lambda e: e.activation(out=lsps[s][0][:, hf * 512:(hf + 1) * 512], in_=tpre[:], func=AF.Ln,
                                                       bias=1.0), reads=[b_tpre], writes=[lsps[s][1]])
            ck("lsp")
            def g_qkv(h):
                wt, bw = next_tile("qk")
                for fb in range(4):
                    dstT, bd = (qT, b_qT) if fb < 2 else (kT, b_kT)

                    def ev(ps, bps, fb=fb, dstT=dstT, bd=bd):
                        S.op("act", lambda e: e.copy(out=dstT[:, fb % 2, :], in_=ps[:, 0:TT]), reads=[bps], writes=[bd])
                    proj_fm(wt, bw, fb * 128, 128, ev)
                for s in range(NS):
                    def ev(ps, bps, s=s):
                        S.op("act", lambda e: e.copy(out=ktok[:, s, :], in_=ps[:, 0:256]), reads=[bps], writes=[b_ktok])
                    proj_tm(wt, bw, uT, b_uT, s, 256, 256, ev)
                ck("qk")
                wt, bw = next_tile("v")
                for s in range(NS):
                    def ev(ps, bps, s=s):
                        S.op("act", lambda e: e.copy(out=vtok[:, s, :], in_=ps[:, :]), reads=[bps], writes=[b_vtok])
                    proj_tm(wt, bw, uT, b_uT, s, 0, 512, ev)
            def g_g(h):
                wt, bw = next_tile("g")
                for s in range(NS):
                    def ev(ps, bps, s=s):
                        S.op("act", lambda e: e.activation(out=tm_b[:, s, :], in_=ps[:, :], func=AF.Silu),
                             reads=[bps], writes=[b_tmb])
                    proj_tm(wt, bw, uT, b_uT, s, 0, 512, ev)
            def g_scan(h):
                for s in range(NS):
                    Eq, b_Eq = T[f"Eq{s}"]
                    Ek, b_Ek = T[f"Ek{s}"]
                    Ed, b_Ed = T[f"Ed{s}"]
                    qtl, b_qtl = T[f"qtl{s}"]
                    ktl, b_ktl = T[f"ktl{s}"]
                    kdec, b_kdec = T[f"kdec{s}"]
                    attT, b_attT = T[f"attT{s}"]
                    sl = slice(s * 128, (s + 1) * 128)
                    lh = lsps[s][0][:, h * 256:(h + 1) * 256]
                    b_lsp_s = lsps[s][1]
                    pc, bpc = psum()
                    for dc in range(2):
                        S.op("pe", lambda e: e.matmul(pc[:, dc * 128:(dc + 1) * 128], lhsT=lh[:, dc * 128:(dc + 1) * 128],
                                                      rhs=tri[:], start=True, stop=True),
                             reads=[b_lsp_s, b_tri], writes=[bpc], signal=False)
                    S.op("pe", lambda e: e.matmul(pc[:, 256:512], lhsT=triU[:], rhs=lh, start=True, stop=True),
                         reads=[b_lsp_s, b_triU], writes=[bpc])
                    S.op("act", lambda e: e.activation(out=Eq[:].rearrange("p a b -> p (a b)"), in_=pc[:, 0:256],
                                                       func=AF.Exp, scale=-1.0 / 16), reads=[bpc], writes=[b_Eq])
                    S.op("act", lambda e: e.activation(out=Ek[:].rearrange("p a b -> p (a b)"), in_=pc[:, 0:256],
                                                       func=AF.Exp, scale=1.0 / 16), reads=[bpc], writes=[b_Ek])
                    S.op("act", lambda e: e.activation(out=Ed[:], in_=pc[:, 256:512], func=AF.Exp, scale=-1.0 / 16),
                         reads=[bpc], writes=[b_Ed])
                    ck("s1")
                    S.op("dve", lambda e: e.scalar_tensor_tensor(out=qtl[:], in0=qT[:, :, sl], scalar=1.0 / 16, in1=Eq[:],
                                                                 op0=ALU.mult, op1=ALU.mult),
                         reads=[b_qT, b_Eq], writes=[b_qtl])
                    S.op("dve", lambda e: e.tensor_tensor(out=ktl[:], in0=kT[:, :, sl], in1=Ek[:], op=ALU.mult),
                         reads=[b_kT, b_Ek], writes=[b_ktl])
                    S.op("dve", lambda e: e.tensor_tensor(out=kdec[:], in0=ktok[:, s, :], in1=Ed[:], op=ALU.mult),
                         reads=[b_ktok, b_Ed], writes=[b_kdec])
                    ck("s2")
                    pa, bpa = psum()
                    for dc in range(2):
                        S.op("pe", lambda e: e.matmul(pa[:, 0:128], lhsT=ktl[:, dc, :], rhs=qtl[:, dc, :],
                                                      start=(dc == 0), stop=(dc == 1)),
                             reads=[b_ktl, b_qtl], writes=[bpa], signal=(dc == 1))
                    S.op("dve", lambda e: e.tensor_tensor(out=attT[:], in0=pa[:, 0:128], in1=tri[:], op=ALU.mult),
                         reads=[bpa, b_tri], writes=[b_attT])
                for s in range(NS):
                    Eq, b_Eq = T[f"Eq{s}"]
                    Ek, b_Ek = T[f"Ek{s}"]
                    Ed, b_Ed = T[f"Ed{s}"]
                    qtl, b_qtl = T[f"qtl{s}"]
                    ktl, b_ktl = T[f"ktl{s}"]
                    kdec, b_kdec = T[f"kdec{s}"]
                    attT, b_attT = T[f"attT{s}"]
                    sl = slice(s * 128, (s + 1) * 128)
                    S.op("act", lambda e: e.copy(out=Sb[:], in_=Sg[:, h, :, :]), reads=[b_Sg], writes=[b_Sb])
                    ck("s3")
                    po_, bpo = psum()
                    S.op("pe", lambda e: e.matmul(po_[:, :], lhsT=attT[:], rhs=vtok[:, s, :], start=True, stop=False),
                         reads=[b_attT, b_vtok], writes=[bpo], signal=False)
                    for dc in range(2):
                        S.op("pe", lambda e: e.matmul(po_[:, :], lhsT=qtl[:, dc, :], rhs=Sb[:, dc, :], start=False,
                                                      stop=(dc == 1)),
                             reads=[b_qtl, b_Sb], writes=[bpo], signal=(dc == 1))
                    S.op("dve", lambda e: e.tensor_copy(out=tm_a[:, s, :], in_=po_[:, :]), reads=[bpo], writes=[b_tma])
                    S.op("act", lambda e: e.activation(out=pjunk[:], in_=po_[:, :], func=AF.Square,
                                                       accum_out=ss[:, s:s + 1]), reads=[bpo], writes=[b_pjunk, b_ss])
                    ck("s4")
                    for dc in range(2):
                        pd, bpd = psum()
                        S.op("pe", lambda e: e.matmul(pd[:, :], lhsT=kdec[:, dc * 128:(dc + 1) * 128], rhs=vtok[:, s, :],
                                                      start=True, stop=True), reads=[b_kdec, b_vtok], writes=[bpd])
                        S.op("dve", lambda e: e.scalar_tensor_tensor(out=Sg[:, h, dc, :], in0=Sg[:, h, dc, :],
                                                                     scalar=Eq[:, dc, 127:128], in1=pd[:, :],
                                                                     op0=ALU.mult, op1=ALU.add),
                             reads=[b_Sg, b_Eq, bpd], writes=[b_Sg])
            def g_post(h):
                S.op("act", lambda e: e.activation(out=ss[:, 0:NS], in_=ss[:, 0:NS], func=AF.Ln, scale=1.0 / 512, bias=EPS),
                     reads=[b_ss], writes=[b_ss])
                S.op("act", lambda e: e.activation(out=ss[:, 0:NS], in_=ss[:, 0:NS], func=AF.Exp, scale=-0.5),
                     reads=[b_ss], writes=[b_ss])
                for s in range(NS):
                    S.op("dve", lambda e: e.scalar_tensor_tensor(out=onb[:, s, :], in0=tm_a[:, s, :], scalar=ss[:, s:s + 1],
                                                                 in1=tm_b[:, s, :], op0=ALU.mult, op1=ALU.mult),
                         reads=[b_tma, b_tmb, b_ss], writes=[b_onb])

            def g_post_b(h):
                for s in range(NS):
                    transpose_to_actT(onb, b_onb, s, aT, baT, h * 4, onw[:, 0:4])
            g_qkv(0)
            g_g(0)
            for h in range(4):
                g_scan(h)
                g_post(h)
                if h < 3:
                    g_qkv(h + 1)
                    g_g(h + 1)
                g_post_b(h)
            ck("heads")
            pes.close()
            for nb in range(4):
                outproj("gout", aT, baT, nb)
            ck("gout")

        def ssd(state_only=False):
            so = state_only
            pes = ExitStack()
            dec = [("dtv", [128, NS, 64], F32), ("dtA", [128, NS, 64], F32), ("acum", [128, NS, 64], F32),
                   ("expa", [128, NS, 64], F32), ("nacum", [128, NS, 64], F32), ("pre", [128, 4, TT + 3], F32), ("cacc", [128, 4, TT], F32),
                   ("bcT", [128, 4, TT], BF16), ("xsT", [128, 4, TT], BF16), ("xstok", [128, NS, 512], BF16),
                   ("Btok", [128, NS, 128], BF16), ("Stb", [128, 512], BF16), ("ytmp", [128, 512], F32),
                   ("junk", [128, 512], BF16)]
            for i in range(NS):
                dec += [(f"Rm{i}", [128, 1024], F32), (f"dmat{i}", [128, 8, 128], F32), (f"scT{i}", [128, 8, 128], BF16),
                        (f"cbm{i}", [128, 128], F32), (f"xdt{i}", [128, 512], BF16), (f"xw{i}", [128, 512], BF16),
                        (f"wst{i}", [128, 8], F32), (f"ela{i}", [128, 8], F32), (f"xsd{i}", [128, 512], BF16)]
            T = phase_alloc(pes, dec)
            dtv, b_dtv = T["dtv"]
            dtA, b_dtA = T["dtA"]
            acum, b_acum = T["acum"]
            expa, b_expa = T["expa"]
            nacum, b_nacum = T["nacum"]
            pre, b_pre = T["pre"]
            cacc, b_cacc = T["cacc"]
            b_pres = [Buf(f"pre{i}") for i in range(4)]
            b_caccs = [Buf(f"cacc{i}") for i in range(4)]
            prev_phase_bufs.extend(b_pres + b_caccs)
            bcT, b_bcT = T["bcT"]
            xsT, b_xsT = T["xsT"]
            xstok, b_xstok = T["xstok"]
            Btok, b_Btok = T["Btok"]
            Stb, b_Stb = T["Stb"]
            ytmp, b_ytmp = T["ytmp"]
            pjunk, b_pjunk = T["junk"]

            def conv_tile(kind, blk0, dst, bdst):
                wt, bw = next_tile(kind)
                pend = [None]

                def conv_s(blk, slot):
                    S.op("act", lambda e: e.activation(out=dst[:, slot, :], in_=cacc[:, slot, :], func=AF.Silu,
                                                       bias=cb[:, blk:blk + 1]),
                         reads=[b_caccs[slot], b_const], writes=[bdst])

                for fb in range(4):
                    def ev(ps, bps, fb=fb):
                        blk, slot = blk0 + fb, fb
                        S.op("act", lambda e: e.copy(out=pre[:, slot, 0:3], in_=cs[:, blk, :]), reads=[b_cs],
                             writes=[b_pres[slot]])
                        S.op("act", lambda e: e.copy(out=pre[:, slot, 3:3 + TT], in_=ps[:, 0:TT]), reads=[bps],
                             writes=[b_pres[slot]])
                        S.op("act", lambda e: e.mul(out=cacc[:, slot, :], in_=ps[:, 0:TT], mul=cw[:, blk, 3:4]),
                             reads=[bps, b_const], writes=[b_caccs[slot]])
                        S.op("act", lambda e: e.copy(out=cs[:, blk, :], in_=pre[:, slot, TT:TT + 3]),
                             reads=[b_pres[slot]], writes=[b_cs])
                        if pend[0] is not None:
                            conv_s(*pend[0])
                        for k in range(3):
                            S.op("dve", lambda e: e.scalar_tensor_tensor(out=cacc[:, slot, :], in0=pre[:, slot, k:k + TT],
                                                                         scalar=cw[:, blk, k:k + 1], in1=cacc[:, slot, :],
                                                                         op0=ALU.mult, op1=ALU.add),
                                 reads=[b_pres[slot], b_caccs[slot], b_const], writes=[b_caccs[slot]])
                        pend[0] = (blk, slot)
                    proj_fm(wt, bw, fb * 128, 128, ev)
                conv_s(*pend[0])

            wt, bw = next_tile("dt")
            for s in range(NS):
                def ev(ps, bps, s=s):
                    S.op("dve", lambda e: e.tensor_tensor(out=dtv[:, s, :], in0=ps[:, 0:64], in1=hv[:, 0, :], op=ALU.add),
                         reads=[bps, b_const], writes=[b_dtv])
                proj_tm(wt, bw, uT, b_uT, s, 0, 64, ev)
            S.op("act", lambda e: e.activation(out=dtv[:], in_=dtv[:], func=AF.Exp), reads=[b_dtv], writes=[b_dtv])
            S.op("act", lambda e: e.activation(out=dtv[:], in_=dtv[:], func=AF.Ln, bias=1.0), reads=[b_dtv], writes=[b_dtv])
            for s in range(NS):
                S.op("dve", lambda e: e.tensor_tensor(out=dtA[:, s, :], in0=dtv[:, s, :], in1=negA[:], op=ALU.mult),
                     reads=[b_dtv, b_negA], writes=[b_dtA])
                ps, bps = psum()
                S.op("pe", lambda e: e.matmul(ps[:, 0:64], lhsT=tri[:], rhs=dtA[:, s, :], start=True, stop=True),
                     reads=[b_tri, b_dtA], writes=[bps])
                S.op("dve", lambda e: e.tensor_copy(out=acum[:, s, :], in_=ps[:, 0:64]), reads=[bps], writes=[b_acum])
                S.op("act", lambda e: e.activation(out=expa[:, s, :], in_=ps[:, 0:64], func=AF.Exp), reads=[bps],
                     writes=[b_expa])
            def s_proj(g):
                gp, gi = g // 2, g % 2
                if gi == 0:
                    conv_tile("bc", gp * 4, bcT, b_bcT)
                conv_tile("xs", 16 + g * 4, xsT, b_xsT)
                if not so:
                    wt, bw = next_tile("z")
                    for s in range(NS):
                        def ev(ps, bps, s=s):
                            S.op("act", lambda e: e.activation(out=tm_b[:, s, :], in_=ps[:, :], func=AF.Silu),
                                 reads=[bps], writes=[b_tmb])
                        proj_tm(wt, bw, uT, b_uT, s, 0, 512, ev)

            def s_tr(g):
                gi = g % 2
                for s in range(NS):
                    pt, po, bpt = psumT()
                    for fb in range(4):
                        S.op("pe", lambda e: e.transpose(out=pt[:, po + fb, :], in_=xsT[:, fb, s * 128:(s + 1) * 128],
                                                         identity=ident[:]),
                             reads=[b_xsT, b_id], writes=[bpt], signal=(fb == 3))
                    S.op("act", lambda e: e.copy(out=xstok[:, s, :].rearrange("p (a b) -> p a b", a=4),
                                                 in_=pt[:, po:po + 4, :]), reads=[bpt], writes=[b_xstok])
                    pt, po, bpt = psumT()
                    S.op("pe", lambda e: e.transpose(out=pt[:, po, :], in_=bcT[:, gi * 2, s * 128:(s + 1) * 128],
                                                     identity=ident[:]), reads=[b_bcT, b_id], writes=[bpt])
                    S.op("act", lambda e: e.copy(out=Btok[:, s, :], in_=pt[:, po, :]), reads=[bpt], writes=[b_Btok])

            def emit_Rm(g):
                if so:
                    return
                hs = slice(g * 8, g * 8 + 8)
                for s in range(NS):
                    Rm, b_Rm = T[f"Rm{s}"]
                    S.op("dve", lambda e: e.tensor_tensor(
                        out=Rm[:, :].rearrange("p (a b) -> p a b", a=8),
                        in0=dtA[:, s, hs].unsqueeze(2).to_broadcast([128, 8, 128]),
                        in1=tri[:].unsqueeze(1).to_broadcast([128, 8, 128]), op=ALU.mult),
                        reads=[b_dtA, b_tri], writes=[b_Rm])

            def s_scan(g):
                if True:
                    gi = g % 2
                    hs = slice(g * 8, g * 8 + 8)
                    pas = {}
                    for s in range(NS):
                        Rm, b_Rm = T[f"Rm{s}"]
                        xdt, b_xdt = T[f"xdt{s}"]
                        xsd, b_xsd = T[f"xsd{s}"]
                        if so:
                            pa0, bpa0 = psum()
                            S.op("pe", lambda e: e.matmul(pa0[:, 0:8], lhsT=ones[:], rhs=dtA[:, s, hs], start=True, stop=True),
                                 reads=[b_ones, b_dtA], writes=[bpa0])
                            pas[s] = ((pa0, bpa0),)
                        else:
                            pa0, bpa0 = psum()
                            pa1, bpa1 = psum()
                            S.op("pe", lambda e: e.matmul(pa0[:, :], lhsT=ones[:], rhs=Rm[:, 0:512], start=True, stop=True),
                                 reads=[b_ones, b_Rm], writes=[bpa0])
                            S.op("pe", lambda e: e.matmul(pa1[:, :], lhsT=ones[:], rhs=Rm[:, 512:1024], start=True, stop=True),
                                 reads=[b_ones, b_Rm], writes=[bpa1])
                            pas[s] = ((pa0, bpa0), (pa1, bpa1))
                        S.op("dve", lambda e: e.tensor_tensor(
                            out=xdt[:].rearrange("p (a b) -> p a b", a=8),
                            in0=xstok[:, s, :].rearrange("p (a b) -> p a b", a=8),
                            in1=dtv[:, s, hs].unsqueeze(2).to_broadcast([128, 8, 64]), op=ALU.mult),
                            reads=[b_xstok, b_dtv], writes=[b_xdt])
                        if not so:
                            S.op("dve", lambda e: e.tensor_tensor(
                                out=xsd[:].rearrange("p (a b) -> p a b", a=8),
                                in0=xstok[:, s, :].rearrange("p (a b) -> p a b", a=8),
                                in1=hv[:, 2, hs].unsqueeze(2).to_broadcast([128, 8, 64]), op=ALU.mult),
                                reads=[b_xstok, b_const], writes=[b_xsd])
                    if g < 7:
                        emit_Rm(g + 1)
                    for s in range(NS):
                        dmat, b_dmat = T[f"dmat{s}"]
                        scT, b_scT = T[f"scT{s}"]
                        cbm, b_cbm = T[f"cbm{s}"]
                        xdt, b_xdt = T[f"xdt{s}"]
                        xw, b_xw = T[f"xw{s}"]
                        wst, b_wst = T[f"wst{s}"]
                        ela, b_ela = T[f"ela{s}"]
                        sl = slice(s * 128, (s + 1) * 128)
                        BT = bcT[:, gi * 2, sl]
                        CT = bcT[:, gi * 2 + 1, sl]
                        if so:
                            pa, bpa = pas[s][0]
                            S.op("dve", lambda e: e.tensor_copy(out=ela[:], in_=pa[:, 0:8]), reads=[bpa], writes=[b_ela])
                        for hf, (pa, bpa) in (enumerate(pas[s]) if not so else ()):
                            S.op("dve", lambda e: e.tensor_copy(
                                out=ela[:, hf * 4:(hf + 1) * 4],
                                in_=pa[:, :].rearrange("p (a b) -> p a b", a=4)[:, :, 127]),
                                reads=[bpa], writes=[b_ela])
                            if not so:
                                for q in range(4):
                                    hh = hf * 4 + q
                                    S.op("act", lambda e: e.activation(out=dmat[:, hh, :], in_=pa[:, q * 128:(q + 1) * 128],
                                                                       func=AF.Relu, scale=-1.0,
                                                                       bias=acum[:, s, g * 8 + hh:g * 8 + hh + 1]),
                                         reads=[bpa, b_acum], writes=[b_dmat])
                        if not so:
                            S.op("act", lambda e: e.activation(out=dmat[:], in_=dmat[:], func=AF.Exp, scale=-1.0),
                                 reads=[b_dmat], writes=[b_dmat])
                        S.op("dve", lambda e: e.tensor_tensor(out=wst[:], in0=ela[:], in1=acum[:, s, hs], op=ALU.subtract),
                             reads=[b_ela, b_acum], writes=[b_wst])
                        S.op("act", lambda e: e.activation(out=wst[:], in_=wst[:], func=AF.Exp), reads=[b_wst],
                             writes=[b_wst])
                        S.op("act", lambda e: e.activation(out=ela[:], in_=ela[:], func=AF.Exp), reads=[b_ela],
                             writes=[b_ela])
                        S.op("dve", lambda e: e.tensor_tensor(
                            out=xw[:].rearrange("p (a b) -> p a b", a=8), in0=xdt[:].rearrange("p (a b) -> p a b", a=8),
                            in1=wst[:].unsqueeze(2).to_broadcast([128, 8, 64]), op=ALU.mult),
                            reads=[b_xdt, b_wst], writes=[b_xw])
                        if not so:
                            pcb, bpcb = psum()
                            S.op("pe", lambda e: e.matmul(pcb[:, 0:128], lhsT=BT, rhs=CT, start=True, stop=True),
                                 reads=[b_bcT], writes=[bpcb])
                            S.op("dve", lambda e: e.tensor_tensor(out=cbm[:], in0=pcb[:, 0:128], in1=tri[:], op=ALU.mult),
                                 reads=[bpcb, b_tri], writes=[b_cbm])
                            S.op("dve", lambda e: e.tensor_tensor(
                                out=scT[:], in0=dmat[:], in1=cbm[:].unsqueeze(1).to_broadcast([128, 8, 128]),
                                op=ALU.mult), reads=[b_dmat, b_cbm], writes=[b_scT])
                    for s in range(NS):
                        scT, b_scT = T[f"scT{s}"]
                        xdt, b_xdt = T[f"xdt{s}"]
                        xsd, b_xsd = T[f"xsd{s}"]
                        xw, b_xw = T[f"xw{s}"]
                        ela, b_ela = T[f"ela{s}"]
                        sl = slice(s * 128, (s + 1) * 128)
                        CT = bcT[:, gi * 2 + 1, sl]
                        if not so:
                            S.op("act", lambda e: e.copy(out=Stb[:], in_=St[:, g, :]), reads=[b_St], writes=[b_Stb])
                        S.op("dve", lambda e: e.tensor_tensor(
                            out=St[:, g, :].rearrange("p (a b) -> p a b", a=8),
                            in0=St[:, g, :].rearrange("p (a b) -> p a b", a=8),
                            in1=ela[:].unsqueeze(2).to_broadcast([128, 8, 64]), op=ALU.mult),
                            reads=[b_St, b_ela], writes=[b_St])
                        if not so:
                            py, bpy = psum()
                            for hh in range(8):
                                S.op("pe", lambda e: e.matmul(py[:, hh * 64:(hh + 1) * 64], lhsT=scT[:, hh, :],
                                                              rhs=xdt[:, hh * 64:(hh + 1) * 64], start=True, stop=False),
                                     reads=[b_scT, b_xdt], writes=[bpy], signal=False)
                                S.op("pe", lambda e: e.matmul(py[:, hh * 64:(hh + 1) * 64], lhsT=ident[:],
                                                              rhs=xsd[:, hh * 64:(hh + 1) * 64], start=False, stop=True),
                                     reads=[b_id, b_xsd], writes=[bpy], signal=(hh == 7))
                            pyi, bpyi = psum()
                            S.op("pe", lambda e: e.matmul(pyi[:, :], lhsT=CT, rhs=Stb[:], start=True, stop=True),
                                 reads=[b_bcT, b_Stb], writes=[bpyi])
                        pd, bpd = psum()
                        S.op("pe", lambda e: e.matmul(pd[:, :], lhsT=Btok[:, s, :], rhs=xw[:], start=True, stop=True),
                             reads=[b_Btok, b_xw], writes=[bpd])
                        if not so:
                            S.op("dve", lambda e: e.tensor_tensor(
                                out=ytmp[:].rearrange("p (a b) -> p a b", a=8), in0=pyi[:, :].rearrange("p (a b) -> p a b", a=8),
                                in1=expa[:, s, hs].unsqueeze(2).to_broadcast([128, 8, 64]), op=ALU.mult),
                                reads=[bpyi, b_expa], writes=[b_ytmp])
                            S.op("dve", lambda e: e.tensor_tensor(out=ytmp[:], in0=ytmp[:], in1=py[:, :], op=ALU.add),
                                 reads=[bpy, b_ytmp], writes=[b_ytmp])
                        S.op("dve", lambda e: e.tensor_tensor(out=St[:, g, :], in0=St[:, g, :], in1=pd[:, :], op=ALU.add),
                             reads=[b_St, bpd], writes=[b_St])
                        if not so:
                            S.op("dve", lambda e: e.tensor_tensor(out=tm_a[:, s, :], in0=ytmp[:], in1=tm_b[:, s, :],
                                                                   op=ALU.mult), reads=[b_ytmp, b_tmb], writes=[b_tma])
                            S.op("act", lambda e: e.activation(out=pjunk[:], in_=tm_a[:, s, :], func=AF.Square,
                                                               accum_out=ss[:, s:s + 1]), reads=[b_tma], writes=[b_pjunk, b_ss])

            def s_post(g):
                if True:
                    aT, baT = actT[g // 4]
                    if not so:
                        S.op("act", lambda e: e.activation(out=ss[:, 0:NS], in_=ss[:, 0:NS], func=AF.Ln, scale=1.0 / 512,
                                                           bias=EPS), reads=[b_ss], writes=[b_ss])
                        S.op("act", lambda e: e.activation(out=ss[:, 0:NS], in_=ss[:, 0:NS], func=AF.Exp, scale=-0.5),
                             reads=[b_ss], writes=[b_ss])
                        for s in range(NS):
                            S.op("dve", lambda e: e.tensor_scalar(out=onb[:, s, :], in0=tm_a[:, s, :], scalar1=ss[:, s:s + 1],
                                                                  scalar2=None, op0=ALU.mult),
                                 reads=[b_tma, b_ss], writes=[b_onb])
                            pass

            def s_post_b(g):
                aT, baT = actT[g // 4]
                if not so:
                    for s in range(NS):
                        transpose_to_actT(onb, b_onb, s, aT, baT, (g % 4) * 4, gnw[:, g * 4:(g + 1) * 4])
            emit_Rm(0)
            s_proj(0)
            s_tr(0)
            for g in range(8):
                s_scan(g)
                s_post(g)
                if g < 7:
                    s_proj(g + 1)
                s_post_b(g)
                if g < 7:
                    s_tr(g + 1)
            pes.close()
            if not so:
                for kb in range(2):
                    for nb in range(4):
                        outproj("sout", actT[kb][0], actT[kb][1], nb)

        if os_environ_get('KINFO'):
            print('SBUF bytes remaining/partition:', nc.sbuf_bytes_remaining)
        nwidx = {0: (0, 1), 1: (2, 3)}
        import os

        class _Stop(Exception):
            pass

        def ck(name):
            if os.environ.get("KSTOP") == name:
                raise _Stop()
        build_program.ck = ck
        try:
          for t in range(NT):
              is_main = t >= n_pre
              S.dma("sp", d_x, hres[:], x_d[t * TT:(t + 1) * TT, :].rearrange("(s p) d -> p s d", p=128),
                    writes=[b_hres])
              ck("load")
              for L in layers:
                  norm_to_uT(nwidx[L][0])
                  ck("norm")
                  pre_so = (not is_main) and L == 1
                  if L == 0:
                      gla()
                  else:
                      ssd(state_only=pre_so)
                  if not pre_so:
                      norm_to_uT(nwidx[L][1])
                      mlp()
              if t == n_pre - 1:
                  for (tt_, bb_) in ((Sg, b_Sg), (St, b_St), (cs, b_cs)):
                      flat = tt_[:].rearrange("p a b c -> p (a b c)") if tt_ is Sg else tt_[:].rearrange("p a b -> p (a b)")
                      S.op("dve", lambda e: e.tensor_scalar(out=flat, in0=flat, scalar1=flag[:, 0:1], scalar2=None,
                                                            op0=ALU.mult), reads=[bb_, b_const], writes=[bb_])
              if is_main:
                  tm = t - n_pre
                  for s in range(NS):
                      if final_norm:
                          S.op("act", lambda e: e.activation(out=ubf[:, 0, :], in_=hres[:, s, :], func=AF.Square,
                                                             accum_out=ss[:, s:s + 1]), reads=[b_hres], writes=b_ubfp[0] + [b_ss])
                          S.op("act", lambda e: e.activation(out=ss[:, s:s + 1], in_=ss[:, s:s + 1], func=AF.Ln, scale=1.0 / D,
                                                             bias=EPS), reads=[b_ss], writes=[b_ss])
                          S.op("act", lambda e: e.activation(out=ss[:, s:s + 1], in_=ss[:, s:s + 1], func=AF.Exp, scale=-0.5),
                               reads=[b_ss], writes=[b_ss])
                          S.op("dve", lambda e: e.scalar_tensor_tensor(out=hres[:, s, :], in0=hres[:, s, :], scalar=ss[:, s:s + 1],
                                                                       in1=fnw[:], op0=ALU.mult, op1=ALU.mult),
                               reads=[b_hres, b_ss, b_const], writes=[b_hres])
                          S.dma("sp", d_out, out_d[tm * TT + s * 128: tm * TT + (s + 1) * 128, :], hres[:, s, :],
                                reads=[b_hres])
                      else:
                          S.dma("sp", d_out, out_d[tm * TT + s * 128: tm * TT + (s + 1) * 128, :], hres[:, s, :],
                                reads=[b_hres])
        except _Stop:
            pass
        else:
            assert wstate["next"] == wstate["total"] == wstate["issued"]
        S.wait_all("sp", [d_out])
        S.wait_all("sp", ["pe", "dve", "act", "pool"] + d_w)
    return nc


def small_inputs(inputs):
    f = np.float32
    fm = lambda v: np.ascontiguousarray(v.reshape(-1, 128).T).astype(f)
    normw = np.concatenate([fm(inputs["mixer_norm_w"][0]), fm(inputs["mlp_norm_w"][0]),
                            fm(inputs["mixer_norm_w"][1]), fm(inputs["mlp_norm_w"][1])], axis=1)
    cwf = inputs["ssd_conv_w"][0]
    cbf = inputs["ssd_conv_b"][0]
    blocks = []
    for gp in range(4):
        g0, g1 = 2 * gp, 2 * gp + 1
        for base in (4096 + g0 * 128, 5120 + g0 * 128, 4096 + g1 * 128, 5120 + g1 * 128):
            blocks.append(np.arange(base, base + 128))
    for g in range(8):
        for fb in range(4):
            blocks.append(np.arange(g * 512 + fb * 128, g * 512 + fb * 128 + 128))
    convw = np.stack([cwf[:, b].T for b in blocks], axis=1)
    convb = np.stack([cbf[b] for b in blocks], axis=1)
    return {
        "normw": np.ascontiguousarray(normw, dtype=f),
        "fnw": inputs["final_norm_w"].reshape(1, D).astype(f),
        "onw": fm(inputs["gla_o_norm_w"][0]),
        "gnw": fm(inputs["ssd_gnorm_w"][0]),
        "bgk": inputs["gla_b_gk_up"][0].reshape(1, 1024).astype(f),
        "wgk": np.ascontiguousarray(inputs["gla_w_gk_up"][0], dtype=f),
        "headvec": np.concatenate([inputs["ssd_dt_bias"][0], inputs["ssd_a_log"][0],
                                   inputs["ssd_d_skip"][0]]).reshape(1, 192).astype(f),
        "convw": np.ascontiguousarray(convw.reshape(128, 48 * 4), dtype=f),
        "convb": np.ascontiguousarray(convb, dtype=f),
    }


def run_stage(inputs, xs_per_core, layers, n_pre, n_main, TT, final_norm, flags):
    wstream = build_wstream(inputs, layers)
    nc = build_program(layers, n_pre, n_main, TT, final_norm, int(wstream.size))
    small = small_inputs(inputs)
    in_maps = []
    for c, xc in enumerate(xs_per_core):
        m = dict(small)
        m["x"] = np.ascontiguousarray(xc, dtype=np.float32)
        m["wstream"] = wstream
        m["flag"] = np.full((128, 1), flags[c], np.float32)
        in_maps.append(m)
    res = run_bass_kernel_spmd(nc, in_maps, core_ids=list(range(len(xs_per_core))))
    return [r["out"] for r in res.results]


TT_DEFAULT = 256
FUSED = True


def kernel(**inputs):
    inputs = {k: np.asarray(v) for k, v in inputs.items()}
    x = inputs["x"]
    TT = TT_DEFAULT
    npre = HALF // TT
    nmain = HALF // TT
    flags = [float(c % 2) for c in range(NCORES)]

    def shard(h):
        xs = []
        for c in range(NCORES):
            b, half = c // 2, c % 2
            first = h[b, 0:HALF] if half == 1 else np.zeros((HALF, D), np.float32)
            mine = h[b, half * HALF:(half + 1) * HALF]
            xs.append(np.concatenate([first, mine], axis=0))
        return xs

    def gather(outs):
        o = np.empty((BATCH, SEQ, D), np.float32)
        for c in range(NCORES):
            b, half = c // 2, c % 2
            o[b, half * HALF:(half + 1) * HALF] = outs[c]
        return o

    if FUSED:
        return gather(run_stage(inputs, shard(x), [0, 1], npre, nmain, TT, True, flags))
    h1 = gather(run_stage(inputs, shard(x), [0], npre, nmain, TT, False, flags))
    return gather(run_stage(inputs, shard(h1), [1], npre, nmain, TT, True, flags))
```
